# Optimizing a Trainium2 kernel written in Bass

```python
import math
import jax, jax.numpy as jnp
from jax import lax
import numpy as np

D_MODEL = 1024
BATCH = 4
SEQ = 4096
DEPTH = 2

N_MEM = 256
CONV_WIDTH_CH = D_MODEL // 2
CONV_K = 3
HEAD_DIM = 64
N_Q_HEADS = (D_MODEL // 2) // HEAD_DIM
N_KV_HEADS = 2
GROUP = N_Q_HEADS // N_KV_HEADS
ATTN_WIDTH = N_Q_HEADS * HEAD_DIM
KV_WIDTH = N_KV_HEADS * HEAD_DIM
WINDOW = 128
BLOCK = WINDOW
ROPE_THETA = 10000.0
N_X_HEADS = 4
X_HEAD_DIM = D_MODEL // N_X_HEADS
D_FF = 4 * D_MODEL
EPS = 1e-6
IN_COLS = 3 * CONV_WIDTH_CH + ATTN_WIDTH + 2 * KV_WIDTH

kernel_name = "hybrid_conv_swa_sink_xattn_block"


def rmsnorm(x, g):
    xf = x.astype(jnp.float32)
    y = xf * lax.rsqrt(jnp.mean(xf * xf, axis=-1, keepdims=True) + EPS)
    return (y * g.astype(jnp.float32)).astype(x.dtype)


def rotary_tables(positions, dtype):
    inv_freq = ROPE_THETA ** (-jnp.arange(0, HEAD_DIM, 2, dtype=jnp.float32) / HEAD_DIM)
    ang = positions.astype(jnp.float32)[..., None] * inv_freq
    return jnp.cos(ang)[:, :, None, :].astype(dtype), jnp.sin(ang)[:, :, None, :].astype(dtype)


def apply_rotary(t, cos, sin):
    t1, t2 = jnp.split(t, 2, axis=-1)
    return jnp.concatenate([t1 * cos - t2 * sin, t2 * cos + t1 * sin], axis=-1)


def causal_short_conv(u, w):
    s = u.shape[1]
    up = jnp.pad(u, ((0, 0), (CONV_K - 1, 0), (0, 0)))
    return sum(w[k] * up[:, k:k + s] for k in range(CONV_K))


def sliding_window_attention(q, k, v, sinks):
    b, s = q.shape[0], q.shape[1]
    nb = s // BLOCK
    qb = q.reshape(b, nb, BLOCK, N_KV_HEADS, GROUP, HEAD_DIM)
    kb = k.reshape(b, nb, BLOCK, N_KV_HEADS, HEAD_DIM)
    vb = v.reshape(b, nb, BLOCK, N_KV_HEADS, HEAD_DIM)

    def with_prev(t):
        prev = jnp.pad(t[:, :-1], ((0, 0), (1, 0), (0, 0), (0, 0), (0, 0)))
        return jnp.concatenate([prev, t], axis=2)

    kw, vw = with_prev(kb), with_prev(vb)
    scale = 1.0 / math.sqrt(HEAD_DIM)
    sc = jnp.einsum('bnqhgd,bnkhd->bnhgqk', qb, kw).astype(jnp.float32) * scale
    qi = jnp.arange(BLOCK)[:, None]
    ki = jnp.arange(2 * BLOCK)[None, :]
    diff = BLOCK + qi - ki
    band = (diff >= 0) & (diff < WINDOW)
    blk = jnp.arange(nb)[:, None, None]
    mask = band[None] & ((blk > 0) | (ki >= BLOCK)[None])
    sc = jnp.where(mask[None, :, None, None], sc, -jnp.inf)
    sink = sinks.astype(jnp.float32).reshape(N_KV_HEADS, GROUP)[None, None, :, :, None, None]
    m = jnp.maximum(jnp.max(sc, axis=-1, keepdims=True), sink)
    p = jnp.exp(sc - m)
    denom = jnp.sum(p, axis=-1, keepdims=True) + jnp.exp(sink - m)
    p = (p / denom).astype(v.dtype)
    o = jnp.einsum('bnhgqk,bnkhd->bnqhgd', p, vw)
    return o.reshape(b, s, ATTN_WIDTH)


def memory_cross_attention(h, memn, wq, wkv, wo):
    b, s, _ = h.shape
    q = (h @ wq).reshape(b, s, N_X_HEADS, X_HEAD_DIM)
    kv = memn @ wkv
    k, v = jnp.split(kv, 2, axis=-1)
    k = k.reshape(b, -1, N_X_HEADS, X_HEAD_DIM)
    v = v.reshape(b, -1, N_X_HEADS, X_HEAD_DIM)
    sc = jnp.einsum('bshd,bmhd->bhsm', q, k).astype(jnp.float32) / math.sqrt(X_HEAD_DIM)
    p = jax.nn.softmax(sc, axis=-1).astype(v.dtype)
    o = jnp.einsum('bhsm,bmhd->bshd', p, v).reshape(b, s, D_MODEL)
    return o @ wo


def setup_inputs(seed: int = 0) -> dict:
    key = jax.random.key(seed)
    ks = jax.random.split(key, 20)
    f32 = jnp.float32

    def nrm(k, shape, scale):
        return jax.random.normal(k, shape, f32) * scale

    def gain(k, shape):
        return 1.0 + 0.02 * jax.random.normal(k, shape, f32)

    return {
        "x": jax.random.normal(ks[0], (BATCH, SEQ, D_MODEL), f32),
        "mem": jax.random.normal(ks[1], (BATCH, N_MEM, D_MODEL), f32),
        "positions": jnp.broadcast_to(jnp.arange(SEQ, dtype=jnp.int32), (BATCH, SEQ)),
        "norm_mix_g": gain(ks[2], (DEPTH, D_MODEL)),
        "w_in": nrm(ks[3], (DEPTH, D_MODEL, IN_COLS), D_MODEL ** -0.5),
        "conv_w": nrm(ks[4], (DEPTH, CONV_K, CONV_WIDTH_CH), CONV_K ** -0.5),
        "sinks": nrm(ks[5], (DEPTH, N_Q_HEADS), 0.5),
        "gnorm_conv_g": gain(ks[6], (DEPTH, CONV_WIDTH_CH)),
        "gnorm_attn_g": gain(ks[7], (DEPTH, ATTN_WIDTH)),
        "w_out": nrm(ks[8], (DEPTH, CONV_WIDTH_CH + ATTN_WIDTH, D_MODEL), (CONV_WIDTH_CH + ATTN_WIDTH) ** -0.5),
        "norm_x_g": gain(ks[9], (DEPTH, D_MODEL)),
        "norm_mem_g": gain(ks[10], (DEPTH, D_MODEL)),
        "wx_q": nrm(ks[11], (DEPTH, D_MODEL, D_MODEL), D_MODEL ** -0.5),
        "wx_kv": nrm(ks[12], (DEPTH, D_MODEL, 2 * D_MODEL), D_MODEL ** -0.5),
        "wx_o": nrm(ks[13], (DEPTH, D_MODEL, D_MODEL), D_MODEL ** -0.5),
        "norm_mlp_g": gain(ks[14], (DEPTH, D_MODEL)),
        "w_up": nrm(ks[15], (DEPTH, D_MODEL, D_FF), D_MODEL ** -0.5),
        "w_down": nrm(ks[16], (DEPTH, D_FF, D_MODEL), D_FF ** -0.5),
        "final_g": gain(ks[17], (D_MODEL,)),
    }


def reference(x, mem, positions, norm_mix_g, w_in, conv_w, sinks, gnorm_conv_g,
              gnorm_attn_g, w_out, norm_x_g, norm_mem_g, wx_q, wx_kv, wx_o,
              norm_mlp_g, w_up, w_down, final_g):
    b, s, _ = x.shape
    cos, sin = rotary_tables(positions, x.dtype)
    splits = np.cumsum([CONV_WIDTH_CH, CONV_WIDTH_CH, CONV_WIDTH_CH, ATTN_WIDTH, KV_WIDTH]).tolist()
    for l in range(DEPTH):
        h = rmsnorm(x, norm_mix_g[l])
        u = h @ w_in[l]
        gb, gc, xc, q, k, v = jnp.split(u, splits, axis=-1)
        conv_out = gb * causal_short_conv(gc * xc, conv_w[l])
        q = apply_rotary(q.reshape(b, s, N_Q_HEADS, HEAD_DIM), cos, sin)
        k = apply_rotary(k.reshape(b, s, N_KV_HEADS, HEAD_DIM), cos, sin)
        v = v.reshape(b, s, N_KV_HEADS, HEAD_DIM)
        attn_out = sliding_window_attention(q, k, v, sinks[l])
        mixed = jnp.concatenate([rmsnorm(conv_out, gnorm_conv_g[l]),
                                 rmsnorm(attn_out, gnorm_attn_g[l])], axis=-1)
        x = x + mixed @ w_out[l]
        x = x + memory_cross_attention(rmsnorm(x, norm_x_g[l]), rmsnorm(mem, norm_mem_g[l]),
                                       wx_q[l], wx_kv[l], wx_o[l])
        hm = rmsnorm(x, norm_mlp_g[l])
        x = x + jnp.square(jax.nn.relu(hm @ w_up[l])) @ w_down[l]
    return rmsnorm(x, final_g)
```

```python
import math
from contextlib import ExitStack

import numpy as np
import concourse.bass as bass
import concourse.mybir as mybir
from concourse.bass_utils import run_bass_kernel_spmd

F32 = mybir.dt.float32
BF = mybir.dt.bfloat16
I32 = mybir.dt.int32
AF = mybir.ActivationFunctionType
ALU = mybir.AluOpType

NCORES = 8
DEPTH = 2
D = 1024
SEQ = 4096
TOK = 2048
HALO = 256
T = TOK + HALO
NB = T // 128
NMEM = 256
DFF = 4096
EPS = 1e-6
TWO_PI = 2.0 * math.pi

C_EPS, C_HPI, C_INVF, C_SIGN, C_FLAG = 0, 1, 2, 3, 4
C_L0 = 8
C_LSTRIDE = 52
O_MIX, O_X, O_MEM, O_MLP, O_GC, O_GA, O_CW = 0, 8, 16, 24, 32, 36, 40
C_FIN = C_L0 + DEPTH * C_LSTRIDE
NCST = C_FIN + 8

WIN_COLS = 2304


class _Ev:
    __slots__ = ("sem", "key", "val", "eng", "clock")

    def __init__(self, sem, key, val, eng, clock=None):
        self.sem, self.key, self.val, self.eng = sem, key, val, eng
        self.clock = clock


class _Eng:
    def __init__(self, name, h, sem, key):
        self.name, self.h, self.sem, self.key = name, h, sem, key
        self.n = 0
        self.seen = {}
        self.last_clock = None


class Trk:
    def __init__(self, nc, es):
        self.nc, self.es = nc, es
        self.tok = {}
        self.eng = {}
        self.dsem = {}
        self.nsem = 0

    def _newsem(self, name):
        self.nsem += 1
        return self.es.enter_context(self.nc.semaphore(name)), self.nsem

    def add_engine(self, name, h):
        sem, key = self._newsem("e_" + name)
        self.eng[name] = _Eng(name, h, sem, key)

    def add_dsem(self, name):
        sem, key = self._newsem("d_" + name)
        self.dsem[name] = [sem, key, 0]

    def _st(self, t):
        s = self.tok.get(t)
        if s is None:
            s = [None, {}]
            self.tok[t] = s
        return s

    def _wait(self, e, ev):
        if ev is None:
            return
        if ev.eng == e.name and e.name == "pe":
            return
        if e.seen.get(ev.key, 0) >= ev.val:
            return
        e.h.wait_ge(ev.sem, ev.val)
        e.seen[ev.key] = ev.val
        if ev.clock:
            sn = e.seen
            for k, v in ev.clock.items():
                if sn.get(k, 0) < v:
                    sn[k] = v

    def _deps(self, e, R, W):
        for t in R:
            self._wait(e, self._st(t)[0])
        for t in W:
            st = self._st(t)
            self._wait(e, st[0])
            for ev in st[1].values():
                self._wait(e, ev)

    def _done(self, ev, R, W, rkey):
        for t in R:
            self._st(t)[1][rkey] = ev
        for t in W:
            st = self._st(t)
            st[0] = ev
            st[1] = {}

    def op(self, en, fn, R=(), W=()):
        e = self.eng[en]
        psr = [t for t in R if isinstance(t, tuple) and t[0] == "ps"]
        if psr:
            R = [t for t in R if not (isinstance(t, tuple) and t[0] == "ps")]
            W = list(W) + psr
        self._deps(e, R, W)
        ins = fn(e.h)
        e.n += 1
        ins.then_inc(e.sem, 1)
        e.last_clock = dict(e.seen)
        self._done(_Ev(e.sem, e.key, e.n, en, e.last_clock), R, W, en)

    def mm(self, out_ap, pairs, R=(), W=()):
        e = self.eng["pe"]
        for t in W:
            if isinstance(t, tuple) and t[0] == "ps":
                ev = self._st(t)[0]
                if ev is None or ev.eng not in ("act", "dve") or e.seen.get(ev.key, 0) >= ev.val:
                    continue
                evn = self._st(("ps", (t[1] + 1) % 8))[0]
                if evn is not None and evn.key == ev.key and ev.val < evn.val <= ev.val + 6:
                    self._wait(e, evn)
        self._deps(e, R, W)
        n = len(pairs)
        ins = None
        for i, (l, r) in enumerate(pairs):
            ins = e.h.matmul(out_ap, l, r, start=(i == 0), stop=(i == n - 1))
        e.n += 1
        ins.then_inc(e.sem, 1)
        e.last_clock = dict(e.seen)
        self._done(_Ev(e.sem, e.key, e.n, "pe", e.last_clock), R, W, "pe")

    def dma(self, qn, out, in_, sem, R=(), W=()):
        e = self.eng[qn]
        self._deps(e, R, W)
        d = self.dsem[sem]
        ins = e.h.dma_start(out=out, in_=in_)
        d[2] += 16
        ins.then_inc(d[0], 16)
        self._done(_Ev(d[0], d[1], d[2], "dma"), R, W, ("dma", sem))

    def finalize_sem(self, sem):
        d = self.dsem[sem]
        for st in self.tok.values():
            if st[0] is not None and st[0].key == d[1]:
                st[0] = _Ev(d[0], d[1], d[2], "dma")
            for k, ev in list(st[1].items()):
                if ev.key == d[1]:
                    st[1][k] = _Ev(d[0], d[1], d[2], "dma")

    def barrier(self, names):
        evs = []
        for n in names:
            e = self.eng[n]
            if e.n > 0:
                evs.append(_Ev(e.sem, e.key, e.n, n, e.last_clock))
        for n in names:
            e = self.eng[n]
            for ev in evs:
                if ev.eng != n:
                    self._wait(e, ev)
                elif n != "pe":
                    self._wait(e, ev)


def build_program():
    nc = bass.Bass("TRN2", target_bir_lowering=False)

    def din(name, shape, dt):
        return nc.dram_tensor(name, shape, dt, kind="ExternalInput").ap()

    xT_d = din("xT", [D, T], F32)
    pos_d = din("pos", [1, T], I32)
    memT_d = din("memT", [D, NMEM], F32)
    cst_d = din("cst", [128, NCST], F32)
    sinks_d = din("sinks", [1, 16], F32)
    masks_d = din("masks", [128, 3 * 512 + 256], F32)
    w_in_d = din("w_in", [DEPTH, 128, 8, WIN_COLS], F32)
    w_out_d = din("w_out", [DEPTH, 128, 8, D], F32)
    wx_q_d = din("wx_q", [DEPTH, 128, 8, D], F32)
    wx_kv_d = din("wx_kv", [DEPTH, 128, 8, 2 * D], F32)
    wx_o_d = din("wx_o", [DEPTH, 128, 8, D], F32)
    w_up_d = din("w_up", [DEPTH, 128, 8, DFF], F32)
    w_down_d = din("w_down", [DEPTH, 128, 32, D], F32)
    out_d = nc.dram_tensor("outT", [D, TOK], F32, kind="ExternalOutput").ap()

    with ExitStack() as es:
        def sb(name, shape, dt):
            return es.enter_context(nc.sbuf_tensor(name, shape, dt))

        xT = sb("xT_sb", [128, 8, T], F32)
        cosT = sb("cosT", [128, T], BF)
        sinT = sb("sinT", [128, T], BF)
        mski = sb("mski", [128, 3 * 512 + 256], BF)
        msk = mski[:, 0:1536].rearrange("p (a b) -> p a b", a=3)
        ident = mski[:, 1536:1664]
        permM = mski[:, 1664:1792]
        qb = sb("qb", [128, 3, 256], BF)
        dg = sb("dg", [128, 12, 128], BF)
        cst = sb("cst_sb", [128, NCST], F32)
        sinkexp = sb("sinkexp", [128, 16], F32)
        ones_d = sb("ones_d", [128, 128], BF)
        ones_g = sb("ones_g", [128, 128], BF)
        Wr = sb("Wr", [128, 8, 4096], BF)
        SBYTES = 53248
        S = sb("S", [128, SBYTES // 2], BF)
        psb = [es.enter_context(nc.psum_tensor(f"ps{i}", [128, 512], F32)) for i in range(8)]

        tk = Trk(nc, es)
        tk.add_engine("pe", nc.tensor)
        tk.add_engine("act", nc.scalar)
        tk.add_engine("dve", nc.vector)
        tk.add_engine("pool", nc.gpsimd)
        tk.add_engine("sp", nc.sync)
        for i in range(8):
            tk.add_dsem(f"w{i}")
        tk.add_dsem("ld")
        tk.add_dsem("xa")
        tk.add_dsem("xb")
        tk.add_dsem("xc")
        tk.add_dsem("ldm")
        tk.add_dsem("mem")
        for i in range(3):
            tk.add_dsem(f"o{i}")

        COMPUTE = ("pe", "act", "dve", "sp")

        class Carve:
            def __init__(self):
                self.off = 0

            def get(self, shape, dt):
                esz = 4 if dt in (F32, I32) else 2
                n = int(np.prod(shape))
                nb = n * esz
                a = self.off
                self.off += (nb + 63) // 64 * 64
                assert self.off <= SBYTES, (self.off, SBYTES)
                ap = S[:, a // 2:(a + nb) // 2]
                if dt != BF:
                    ap = ap.bitcast(dt)
                if len(shape) == 2:
                    return ap.rearrange("p (a b) -> p a b", a=shape[0])
                if len(shape) == 3:
                    return ap.rearrange("p (a b c) -> p a b c", a=shape[0], b=shape[1])
                return ap

        psctr = [0]

        ps_held = set()

        def next_ps(hold=False):
            while True:
                i = psctr[0] % 8
                psctr[0] += 1
                if i not in ps_held:
                    break
            if hold:
                ps_held.add(i)
            return psb[i], ("ps", i)

        def ps_release(ptok):
            ps_held.discard(ptok[1])

        def col(c):
            return cst[:, c:c + 1]

        def xtoks(kc, t0, n):
            return [("x", kc, j) for j in range(t0 // 256, (t0 + n + 255) // 256)]

        def wslot(slot, kcn, cols):
            return Wr[:, slot, 0:kcn * cols].rearrange("p (k c) -> p k c", k=kcn)

        def load_w(slot, src_ap, kcn, cols):
            dst = wslot(slot, kcn, cols)
            h = kcn // 2
            tk.dma("pool", dst[:, 0:h, :], src_ap[:, 0:h, :], f"w{slot}", W=[("W", slot)])
            tk.dma("pool", dst[:, h:kcn, :], src_ap[:, h:kcn, :], f"w{slot}", W=[("W", slot)])

        tk.dma("sp", cst[:], cst_d[:], "ld", W=["cst"])
        cv = Carve()
        posi = cv.get([T], I32)
        angf = cv.get([T], F32)
        tmpf = cv.get([T], F32)
        tmpi = cv.get([T], I32)
        rr = cv.get([T], F32)
        sinks_b = cv.get([16], F32)
        maskf = None
        tk.dma("sp", posi, pos_d.partition_broadcast(128), "ld", W=["posi"])
        tk.dma("sp", sinks_b, sinks_d.partition_broadcast(128), "ld", W=["sinks_b"])
        tk.finalize_sem("ld")
        for kc in range(8):
            tk.dma("sp", xT[:, kc, 0:256], xT_d[kc * 128:(kc + 1) * 128, 0:256], "xa", W=[("x", kc, 0)])
        tk.finalize_sem("xa")
        tk.dma("pool", mski[:], masks_d[:], "ldm", W=["msk"])

        tk.op("dve", lambda h: h.memset(ones_d[:], 1.0 / D), W=["ones_d"])
        tk.op("dve", lambda h: h.memset(ones_g[:], 1.0 / 512.0), W=["ones_g"])
        tk.op("dve", lambda h: h.tensor_copy(out=angf, in_=posi), R=["posi"], W=["angf"])
        tk.op("dve", lambda h: h.tensor_scalar(out=angf, in0=angf, scalar1=col(C_INVF), scalar2=None, op0=ALU.mult),
              R=["cst"], W=["angf"])
        for which in (0, 1):
            add = 0.0 if which == 0 else 0.25
            tk.op("dve", lambda h: h.tensor_scalar(out=tmpf, in0=angf, scalar1=1.0 / TWO_PI, scalar2=add,
                                                   op0=ALU.mult, op1=ALU.add), R=["angf"], W=["tmpf"])
            tk.op("dve", lambda h: h.tensor_copy(out=tmpi, in_=tmpf), R=["tmpf"], W=["tmpi"])
            tk.op("dve", lambda h: h.tensor_copy(out=tmpf, in_=tmpi), R=["tmpi"], W=["tmpf"])
            tk.op("dve", lambda h: h.scalar_tensor_tensor(out=rr, in0=tmpf, scalar=-TWO_PI, in1=angf,
                                                          op0=ALU.mult, op1=ALU.add), R=["tmpf", "angf"], W=["rr"])
            if which == 0:
                lo, hi = -3.14159, 3.14159
            else:
                lo, hi = -3.14159 - math.pi / 2, 3.14159 - math.pi / 2
            tk.op("dve", lambda h: h.tensor_scalar(out=rr, in0=rr, scalar1=lo, scalar2=hi, op0=ALU.max, op1=ALU.min),
                  W=["rr"])
            if which == 0:
                tk.op("act", lambda h: h.activation(out=sinT[:], in_=rr, func=AF.Sin, scale=col(C_SIGN)),
                      R=["rr", "cst"], W=["sinT"])
            else:
                tk.op("act", lambda h: h.activation(out=cosT[:], in_=rr, func=AF.Sin, bias=col(C_HPI), scale=1.0),
                      R=["rr", "cst"], W=["cosT"])
        tk.op("act", lambda h: h.activation(out=sinkexp[:], in_=sinks_b, func=AF.Exp), R=["sinks_b"], W=["sinkexp"])
        tk.barrier(COMPUTE)

        def sq_part(srcs, sq, sqn, n):
            for i, (ap, toks) in enumerate(srcs):
                tk.op("act", lambda h, ap=ap, i=i: h.activation(out=sq[:, i, 0:n], in_=ap, func=AF.Square),
                      R=toks, W=[(sqn, i)])

        def rstd_part(ns, ones_ap, ones_tok, n, sq, sqn, lnb, lnbn, rstd, rstd_tok):
            pt, ptok = next_ps()
            tk.mm(pt[:, 0:n], [(ones_ap, sq[:, i, 0:n]) for i in range(ns)],
                  R=[ones_tok] + [(sqn, i) for i in range(ns)], W=[ptok])
            tk.op("act", lambda h: h.activation(out=lnb[:, 0:n], in_=pt[:, 0:n], func=AF.Ln, bias=col(C_EPS), scale=1.0),
                  R=[ptok, "cst"], W=[lnbn])
            tk.op("act", lambda h: h.activation(out=rstd[:, 0:n], in_=lnb[:, 0:n], func=AF.Exp, scale=-0.5),
                  R=[lnbn], W=[rstd_tok])

        def rms_rstd(srcs, ones_ap, ones_tok, n, sq, lnb, rstd, rstd_tok, sqn="sq", lnbn="lnb"):
            sq_part(srcs, sq, sqn, n)
            rstd_part(len(srcs), ones_ap, ones_tok, n, sq, sqn, lnb, lnbn, rstd, rstd_tok)

        def xsrcs(t0, n):
            return [(xT[:, kc, t0:t0 + n], xtoks(kc, t0, n)) for kc in range(8)]

        def h_part(t0, n, gcol0, rstd, hdst, htoks, rstd_tok="rstd"):
            for kc in range(8):
                tk.op("dve", lambda h, kc=kc: h.scalar_tensor_tensor(
                    out=hdst[:, kc, 0:n], in0=xT[:, kc, t0:t0 + n], scalar=col(gcol0 + kc), in1=rstd[:, 0:n],
                    op0=ALU.mult, op1=ALU.mult), R=xtoks(kc, t0, n) + [rstd_tok, "cst"], W=[htoks[kc]])

        def norm_h(t0, n, gcol0, sq, lnb, rstd, hdst, htoks):
            rms_rstd(xsrcs(t0, n), ones_d[:], "ones_d", n, sq, lnb, rstd, "rstd")
            h_part(t0, n, gcol0, rstd, hdst, htoks)

        def proj_residual(wslots, hsrc, htoks, t0, n, nk=8):
            for oc in range(8):
                slot = wslots[oc // 4]
                w = wslot(slot, nk, 512)
                c0 = (oc % 4) * 128
                pt, ptok = next_ps()
                tk.mm(pt[:, 0:n], [(w[:, kc, c0:c0 + 128], hsrc[:, kc, 0:n]) for kc in range(nk)],
                      R=[("W", slot)] + list(htoks), W=[ptok])
                tk.op("dve", lambda h, oc=oc, pt=pt: h.tensor_tensor(
                    out=xT[:, oc, t0:t0 + n], in0=pt[:, 0:n], in1=xT[:, oc, t0:t0 + n], op=ALU.add),
                    R=[ptok], W=xtoks(oc, t0, n))

        for l in range(DEPTH):
            cb = C_L0 + l * C_LSTRIDE
            for s, (c0, nc_) in enumerate([(0, 512), (512, 512), (1024, 512), (1536, 512), (2048, 256)]):
                load_w(s, w_in_d[l][:, :, c0:c0 + nc_], 8, nc_)
            load_w(6, w_out_d[l][:, :, 0:512], 8, 512)
            load_w(7, w_out_d[l][:, :, 512:1024], 8, 512)
            if l == 0:
                tk._wait(tk.eng["sp"], tk._st(("W", 2))[0])
                for kc in range(8):
                    tk.dma("sp", xT[:, kc, 256:768], xT_d[kc * 128:(kc + 1) * 128, 256:768], "xb",
                           W=[("x", kc, 1), ("x", kc, 2)])
                tk.finalize_sem("xb")
                tk._wait(tk.eng["sp"], tk._st(("W", 7))[0])
                for kc in range(8):
                    tk.dma("sp", xT[:, kc, 768:T], xT_d[kc * 128:(kc + 1) * 128, 768:T], "xc",
                           W=[("x", kc, j) for j in range(3, T // 256)])
                tk.finalize_sem("xc")

            cv = Carve()
            N1 = 256
            sq = cv.get([8, N1], BF)
            sqg = cv.get([8, N1], BF)
            lnb = cv.get([1, N1], F32)[:, 0, :]
            lnb2 = cv.get([1, N1], F32)[:, 0, :]
            rstd = cv.get([1, N1], F32)[:, 0, :]
            hb = cv.get([8, N1], BF)
            gcs = cv.get([2, N1], F32)
            pbuf = cv.get([4, N1 + 4], BF)
            ybuf = cv.get([2, N1], F32)
            cvo = cv.get([4, N1], F32)
            rt = cv.get([4, N1], F32)
            qrot = cv.get([4, N1], BF)
            krot = cv.get([3, 128], BF)
            Vr = cv.get([3, 4, 64], BF)
            Eb = cv.get([4, 512], BF)
            den = cv.get([2, 512], F32)
            attn = cv.get([4, N1], F32)
            rstd2 = cv.get([2, N1], F32)
            mixed = cv.get([8, N1], BF)

            tk.op("dve", lambda h: h.memset(pbuf[:, :, 0:2], 0.0), W=[("p", c) for c in range(4)])
            for k in range(3):
                for c in range(4):
                    tk.op("dve", lambda h, k=k, c=c: h.tensor_scalar(
                        out=dg[:, k * 4 + c, :], in0=ident, scalar1=col(cb + O_CW + k * 4 + c), scalar2=None,
                        op0=ALU.mult), R=["msk", "cst"], W=[("dg", k * 4 + c)])
            tk.op("dve", lambda h: h.memset(krot[:], 0.0), W=[("krot", i) for i in range(3)])
            tk.op("dve", lambda h: h.memset(Vr[:, :, 0:4:3, :], 0.0), W=[("Vr", i) for i in range(3)])
            tk.op("dve", lambda h: h.memset(Vr[:, :, 1:3, :], 1.0), W=[("Vr", i) for i in range(3)])

            wgc, wxc, wgb = wslot(0, 8, 512), wslot(1, 8, 512), wslot(2, 8, 512)
            wq = wslot(3, 8, 512)
            wkv = wslot(4, 8, 256)
            htoks = [("h", kc) for kc in range(8)]

            if l == 0:
                tiles1 = [(i * N1, N1, False) for i in range(T // N1)]
            else:
                tiles1 = [(128, 128, True)] + [(i * N1, N1, False) for i in range(1, T // N1)]

            def inproj(w, slot, c0, n, hoff=0, hold=False):
                pt, ptok = next_ps(hold)
                tk.mm(pt[:, 0:n], [(w[:, kc, c0:c0 + 128], hb[:, kc, hoff:hoff + n]) for kc in range(8)],
                      R=[("W", slot)] + htoks, W=[ptok])
                return pt, ptok

            def p1_sq(td):
                t0, n, _ = td
                sq_part(xsrcs(t0, n), sq, "sq", n)

            def p1_normrest(td):
                t0, n, _ = td
                rstd_part(8, ones_d[:], "ones_d", n, sq, "sq", lnb, "lnb", rstd, "rstd")
                h_part(t0, n, cb + O_MIX, rstd, hb, htoks)

            gbps = {}

            def p1_conv_A(td, c, first):
                t0, n, kvonly = td
                g = c % 2
                pa, pat = inproj(wgc, 0, c * 128, n)
                tk.op("act", lambda h: h.copy(out=gcs[:, g, 0:n], in_=pa[:, 0:n]), R=[pat], W=[("gcs", g)])
                pbk, pbt = inproj(wxc, 1, c * 128, n)
                if first:
                    tk.op("dve", lambda h: h.scalar_tensor_tensor(
                        out=pbuf[:, c, 2:2 + n], in0=pbk[:, 0:n], scalar=col(C_FLAG), in1=gcs[:, g, 0:n],
                        op0=ALU.mult, op1=ALU.mult), R=[pbt, ("gcs", g), "cst"], W=[("p", c)])
                else:
                    tk.op("dve", lambda h: h.tensor_tensor(
                        out=pbuf[:, c, 2:2 + n], in0=pbk[:, 0:n], in1=gcs[:, g, 0:n], op=ALU.mult),
                        R=[pbt, ("gcs", g)], W=[("p", c)])
                if not kvonly:
                    gbps[c] = inproj(wgb, 2, c * 128, n, hold=True)

            def p1_conv_B(td, c):
                t0, n, kvonly = td
                g = c % 2
                if not kvonly:
                    py, pyt = next_ps()
                    tk.mm(py[:, 0:n], [(dg[:, k * 4 + c, :], pbuf[:, c, k:k + n]) for k in range(3)],
                          R=[("p", c)] + [("dg", k * 4 + c) for k in range(3)], W=[pyt])
                    tk.op("act", lambda h: h.copy(out=ybuf[:, g, 0:n], in_=py[:, 0:n]), R=[pyt], W=[("y", g)])
                tk.op("dve", lambda h: h.tensor_copy(out=pbuf[:, c, 0:2], in_=pbuf[:, c, n:n + 2]), W=[("p", c)])
                if not kvonly:
                    pg, pgt = gbps.pop(c)
                    tk.op("dve", lambda h: h.tensor_tensor(
                        out=cvo[:, c, 0:n], in0=pg[:, 0:n], in1=ybuf[:, g, 0:n], op=ALU.mult),
                        R=[pgt, ("y", g)], W=[("cvo", c)])
                    ps_release(pgt)

            def p1_conv_step(td, c, first=False):
                if c < 4:
                    p1_conv_A(td, c, first)
                if c >= 1:
                    p1_conv_B(td, c - 1)

            def rotary(pq, pqt, pqs, pqst, dst, dtok, r, poff, ncols, tt0):
                tk.op("dve", lambda h: h.tensor_tensor(out=rt[:, 2 * r, 0:ncols], in0=pq[:, poff:poff + ncols],
                                                       in1=cosT[:, tt0:tt0 + ncols], op=ALU.mult),
                      R=[pqt, "cosT"], W=[("rt", 2 * r)])
                tk.op("dve", lambda h: h.tensor_tensor(out=rt[:, 2 * r + 1, 0:ncols], in0=pqs[:, poff:poff + ncols],
                                                       in1=sinT[:, tt0:tt0 + ncols], op=ALU.mult),
                      R=[pqst, "sinT"], W=[("rt", 2 * r + 1)])
                tk.op("dve", lambda h: h.tensor_tensor(out=dst, in0=rt[:, 2 * r, 0:ncols],
                                                        in1=rt[:, 2 * r + 1, 0:ncols], op=ALU.add),
                      R=[("rt", 2 * r), ("rt", 2 * r + 1)], W=[dtok])

            def _rot_flush(item, n):
                pq, pqt, qi, outs = item
                pqs, pqst = next_ps()
                tk.mm(pqs[:, 0:n], [(permM, qb[:, qi, 0:n])], R=[("qb", qi), "msk"], W=[pqst])
                for (dst, dtok, r, poff, ncols, tt0) in outs:
                    rotary(pq, pqt, pqs, pqst, dst, dtok, r, poff, ncols, tt0)
                ps_release(pqt)

            def p1_q(td):
                t0, n, kvonly = td
                if kvonly:
                    return
                pend = []
                for j in range(4):
                    pq, pqt = inproj(wq, 3, j * 128, n, hold=True)
                    qi = j % 3
                    tk.op("act", lambda h: h.copy(out=qb[:, qi, 0:n], in_=pq[:, 0:n]), R=[pqt], W=[("qb", qi)])
                    pend.append((pq, pqt, qi, [(qrot[:, j, 0:n], ("qrot", j), j % 2, 0, n, t0)]))
                    if len(pend) == 3:
                        _rot_flush(pend.pop(0), n)
                while pend:
                    _rot_flush(pend.pop(0), n)

            def p1_kv(td):
                t0, n, kvonly = td
                pk, pkt = inproj(wkv, 4, 0, n, hold=True)
                tk.op("act", lambda h: h.copy(out=qb[:, 0, 0:n], in_=pk[:, 0:n]), R=[pkt], W=[("qb", 0)])
                kouts = []
                for b in range(n // 128):
                    B = (t0 + b * 128) // 128
                    ri = B % 3
                    kouts.append((krot[:, ri, :], ("krot", ri), b % 2, b * 128, 128, t0 + b * 128))
                for b in range(n // 128):
                    B = (t0 + b * 128) // 128
                    ri = B % 3
                    pv, pvt = next_ps()
                    tk.mm(pv[:, 0:128], [(hb[:, kc, b * 128:(b + 1) * 128], wkv[:, kc, 128:256]) for kc in range(8)],
                          R=[("W", 4)] + htoks, W=[pvt])
                    tk.op("act", lambda h, pv=pv, ri=ri: h.copy(
                        out=Vr[:, ri, 0:4:3, :], in_=pv[:, 0:128].rearrange("p (a b) -> p a b", a=2)),
                        R=[pvt], W=[("Vr", ri)])
                _rot_flush((pk, pkt, 0, kouts), n)

            def p1_S(td, b, g):
                t0, n, _ = td
                B = t0 // 128 + b
                rc, rp = B % 3, (B - 1) % 3
                orow = slice(64 * g, 64 * g + 64)
                qr = qrot[orow, :, b * 128:(b + 1) * 128]
                qtoks = [("qrot", j) for j in range(4)]
                ets = []
                for which, rix in ((0, rp), (1, rc)):
                    pt, ptok = next_ps()
                    tk.mm(pt[:, :], [(krot[orow, rix, :], qr)], R=[("krot", rix)] + qtoks, W=[ptok])
                    ei = 2 * g + which
                    tk.op("act", lambda h, pt=pt, ei=ei: h.activation(out=Eb[:, ei, :], in_=pt[:, :],
                                                                      func=AF.Exp, scale=0.125),
                          R=[ptok], W=[("E", ei)])
                    mi = 1 if which == 1 else (2 if B == 2 else 0)
                    tk.op("dve", lambda h, ei=ei, mi=mi: h.tensor_tensor(out=Eb[:, ei, :], in0=Eb[:, ei, :],
                                                                          in1=msk[:, mi, :], op=ALU.mult),
                          R=["msk"], W=[("E", ei)])
                    ets.append((ei, rix))
                return ets

            def p1_PV(td, b, g, ets):
                orow = slice(64 * g, 64 * g + 64)
                srow = slice(64 * (1 - g), 64 * (1 - g) + 64)
                po, pot = next_ps()
                tk.mm(po[:, :], [(Vr[:, rix, 2 * g:2 * g + 2, :].rearrange("p a b -> p (a b)"), Eb[:, ei, :])
                                 for (ei, rix) in ets],
                      R=[("E", ei) for ei, _ in ets] + [("Vr", rix) for _, rix in ets], W=[pot])
                sk = sinkexp[srow, l * 8 + 4 * g:l * 8 + 4 * g + 4].unsqueeze(2).to_broadcast([64, 4, 128])
                tk.op("dve", lambda h: h.tensor_tensor(
                    out=den[orow, b, :].rearrange("p (a b) -> p a b", a=4),
                    in0=po[srow, :].rearrange("p (a b) -> p a b", a=4), in1=sk, op=ALU.add),
                    R=[pot, "sinkexp"], W=[("den", b, g)])
                tk.op("act", lambda h: h.copy(out=attn[orow, :, b * 128:(b + 1) * 128],
                                              in_=po[orow, :].rearrange("p (a b) -> p a b", a=4)),
                      R=[pot], W=[("attn", g, b)])

            def p1_attn_norm(td):
                for b in range(2):
                    dt_ = [("den", b, 0), ("den", b, 1)]
                    tk.op("act", lambda h: h.activation(out=den[:, b, :], in_=den[:, b, :], func=AF.Ln), W=dt_)
                    tk.op("act", lambda h: h.activation(out=den[:, b, :], in_=den[:, b, :], func=AF.Exp, scale=-1.0), W=dt_)
                    tk.op("dve", lambda h: h.tensor_tensor(
                        out=attn[:, :, b * 128:(b + 1) * 128], in0=attn[:, :, b * 128:(b + 1) * 128],
                        in1=den[:, b, :].rearrange("p (a b) -> p a b", a=4), op=ALU.mult),
                        R=dt_, W=[("attn", 0, b), ("attn", 1, b)])

            def p1_attn(td, nx):
                u = [(b, g) for b in range(2) for g in range(2)]

                def cstep(c):
                    if nx is not None:
                        p1_conv_step(nx, c)

                e0 = p1_S(td, *u[0])
                e1 = p1_S(td, *u[1])
                cstep(0)
                p1_PV(td, u[0][0], u[0][1], e0)
                e2 = p1_S(td, *u[2])
                p1_gconv_rest(td)
                cstep(1)
                p1_PV(td, u[1][0], u[1][1], e1)
                e3 = p1_S(td, *u[3])
                cstep(2)
                p1_PV(td, u[2][0], u[2][1], e2)
                p1_PV(td, u[3][0], u[3][1], e3)
                cstep(3)
                p1_attn_norm(td)
                cstep(4)

            atoks_all = [("attn", g, b) for g in range(2) for b in range(2)]

            def p1_gconv_sq(td):
                n = td[1]
                sq_part([(cvo[:, c, 0:n], [("cvo", c)]) for c in range(4)], sqg, "sqg", n)

            def p1_gconv_rest(td):
                n = td[1]
                rstd_part(4, ones_g[:], "ones_g", n, sqg, "sqg", lnb2, "lnb2", rstd2[:, 0, :], ("rstd2", 0))
                for c in range(4):
                    tk.op("dve", lambda h, c=c: h.scalar_tensor_tensor(
                        out=mixed[:, c, 0:n], in0=cvo[:, c, 0:n], scalar=col(cb + O_GC + c), in1=rstd2[:, 0, 0:n],
                        op0=ALU.mult, op1=ALU.mult), R=[("cvo", c), ("rstd2", 0), "cst"], W=[("mixed", c)])

            def p1_gattn_sq(td):
                n = td[1]
                for j in range(4):
                    tk.op("act", lambda h, j=j: h.activation(out=sqg[:, 4 + j, 0:n], in_=attn[:, j, 0:n], func=AF.Square),
                          R=atoks_all, W=[("sqg", 4 + j)])

            def p1_gattn(td):
                n = td[1]
                pt, ptok = next_ps()
                tk.mm(pt[:, 0:n], [(ones_g[:], sqg[:, 4 + j, 0:n]) for j in range(4)],
                      R=["ones_g"] + [("sqg", 4 + j) for j in range(4)], W=[ptok])
                tk.op("act", lambda h: h.activation(out=lnb2[:, 0:n], in_=pt[:, 0:n], func=AF.Ln, bias=col(C_EPS), scale=1.0),
                      R=[ptok, "cst"], W=["lnb2"])
                tk.op("act", lambda h: h.activation(out=rstd2[:, 1, 0:n], in_=lnb2[:, 0:n], func=AF.Exp, scale=-0.5),
                      R=["lnb2"], W=[("rstd2", 1)])
                for j in range(4):
                    tk.op("dve", lambda h, j=j: h.scalar_tensor_tensor(
                        out=mixed[:, 4 + j, 0:n], in0=attn[:, j, 0:n], scalar=col(cb + O_GA + j), in1=rstd2[:, 1, 0:n],
                        op0=ALU.mult, op1=ALU.mult), R=atoks_all + [("rstd2", 1), "cst"], W=[("mixed", 4 + j)])

            def tl(i):
                return tiles1[i] if i < len(tiles1) else None

            p1_sq(tl(0))
            p1_normrest(tl(0))
            p1_sq(tl(1))
            for c in range(5):
                p1_conv_step(tl(0), c, True)
            p1_q(tl(0))
            p1_kv(tl(0))
            p1_normrest(tl(1))
            p1_sq(tl(2))
            if not tl(0)[2]:
                p1_gconv_sq(tl(0))
            for i, td in enumerate(tiles1):
                nx, nx2, nx3 = tl(i + 1), tl(i + 2), tl(i + 3)
                kvonly = td[2]
                if not kvonly:
                    p1_attn(td, nx)
                elif nx is not None:
                    for c in range(5):
                        p1_conv_step(nx, c)
                if nx is not None:
                    p1_q(nx)
                if not kvonly:
                    p1_gattn_sq(td)
                    p1_gattn(td)
                if nx is not None:
                    p1_kv(nx)
                if nx2 is not None:
                    p1_normrest(nx2)
                if nx3 is not None:
                    p1_sq(nx3)
                if nx is not None:
                    p1_gconv_sq(nx)
                if not kvonly:
                    proj_residual((6, 7), mixed, [("mixed", ii) for ii in range(8)], td[0], td[1])

            tk.barrier(COMPUTE)

            load_w(5, wx_q_d[l][:, :, 512:1024], 8, 512)
            load_w(4, wx_q_d[l][:, :, 0:512], 8, 512)
            for s in range(4):
                load_w(s, wx_kv_d[l][:, :, s * 512:(s + 1) * 512], 8, 512)
            load_w(6, wx_o_d[l][:, :, 0:512], 8, 512)
            load_w(7, wx_o_d[l][:, :, 512:1024], 8, 512)

            cv = Carve()
            N2 = 512
            KT = cv.get([8, NMEM], BF)
            Vm = cv.get([2, D], BF)
            sq = cv.get([8, N2], BF)
            lnb = cv.get([1, N2], F32)[:, 0, :]
            rstd = cv.get([1, N2], F32)[:, 0, :]
            hb = cv.get([8, N2], BF)
            qT = cv.get([8, N2], BF)
            cv_mark = cv.off
            memT = cv.get([8, NMEM], F32)
            memn = cv.get([8, NMEM], BF)
            lnb_m = cv.get([1, NMEM], F32)[:, 0, :]
            rstd_m = cv.get([1, NMEM], F32)[:, 0, :]
            cv.off = cv_mark
            E2 = cv.get([4, N2], BF)
            rec = cv.get([2, N2], F32)
            onb = cv.get([8, N2], BF)
            htoks = [("h", kc) for kc in range(8)]
            tstart = 128 if l == 0 else 256
            if l == 0:
                tiles2 = [(128, 448), (576, 448), (1024, 448), (1472, 448), (1920, 384)]
            else:
                tiles2 = [(256 + 512 * i, 512) for i in range(4)]

            def p2_qproj(t0, n):
                for ch in range(8):
                    slot = 4 + ch // 4
                    w = wslot(slot, 8, 512)
                    c0 = (ch % 4) * 128
                    pt, ptok = next_ps()
                    tk.mm(pt[:, 0:n], [(w[:, kc, c0:c0 + 128], hb[:, kc, 0:n]) for kc in range(8)],
                          R=[("W", slot)] + htoks, W=[ptok])
                    tk.op("act", lambda h, pt=pt, ch=ch: h.copy(out=qT[:, ch, 0:n], in_=pt[:, 0:n]), R=[ptok], W=[("qT", ch)])

            def p2_S(hx, n):
                eis = []
                for mc in range(2):
                    pt, ptok = next_ps()
                    tk.mm(pt[:, 0:n], [(KT[:, hx * 2 + dc, mc * 128:(mc + 1) * 128], qT[:, hx * 2 + dc, 0:n])
                                       for dc in range(2)],
                          R=[("KT", hx * 2), ("KT", hx * 2 + 1), ("qT", hx * 2), ("qT", hx * 2 + 1)], W=[ptok])
                    ei = (hx % 2) * 2 + mc
                    tk.op("act", lambda h, pt=pt, ei=ei: h.activation(out=E2[:, ei, 0:n], in_=pt[:, 0:n], func=AF.Exp,
                                                                      scale=1.0 / 16.0), R=[ptok], W=[("E2", ei)])
                    eis.append(ei)
                return eis

            def p2_RPV(hx, n, eis):
                etoks = [("E2", ei) for ei in eis]
                pr, prt = next_ps()
                tk.mm(pr[:, 0:n], [(ones_d[:], E2[:, ei, 0:n]) for ei in eis], R=etoks + ["ones_d"], W=[prt])
                ri = hx % 2
                tk.op("act", lambda h: h.activation(out=rec[:, ri, 0:n], in_=pr[:, 0:n], func=AF.Ln, scale=float(D)),
                      R=[prt], W=[("rec", ri)])
                tk.op("act", lambda h: h.activation(out=rec[:, ri, 0:n], in_=rec[:, ri, 0:n], func=AF.Exp, scale=-1.0),
                      W=[("rec", ri)])
                for dc in range(2):
                    po, pot = next_ps()
                    tk.mm(po[:, 0:n], [(Vm[:, mc, hx * 256 + dc * 128: hx * 256 + (dc + 1) * 128], E2[:, eis[mc], 0:n])
                                       for mc in range(2)],
                          R=etoks + [("Vm", mc, nh) for mc in range(2) for nh in range(2)], W=[pot])
                    tk.op("dve", lambda h, po=po, dc=dc: h.tensor_tensor(
                        out=onb[:, hx * 2 + dc, 0:n], in0=po[:, 0:n], in1=rec[:, ri, 0:n], op=ALU.mult),
                        R=[pot, ("rec", ri)], W=[("onb", hx * 2 + dc)])

            for kc in range(8):
                tk.dma("sp", memT[:, kc, :], memT_d[kc * 128:(kc + 1) * 128, :], "mem", W=[("memT", kc)])
            tk.finalize_sem("mem")
            t0_, n_ = tiles2[0]
            norm_h(t0_, n_, cb + O_X, sq, lnb, rstd, hb, htoks)
            sq_part([(memT[:, kc, :], [("memT", kc)]) for kc in range(8)], sq, "sq", NMEM)
            rstd_part(8, ones_d[:], "ones_d", NMEM, sq, "sq", lnb_m, "lnb_m", rstd_m, "rstd_m")
            for kc in range(8):
                tk.op("dve", lambda h, kc=kc: h.scalar_tensor_tensor(
                    out=memn[:, kc, :], in0=memT[:, kc, :], scalar=col(cb + O_MEM + kc), in1=rstd_m[:, 0:NMEM],
                    op0=ALU.mult, op1=ALU.mult), R=[("memT", kc), "rstd_m", "cst"], W=[("memn", kc)])
            if len(tiles2) > 1:
                sq_part(xsrcs(tiles2[1][0], tiles2[1][1]), sq, "sq", tiles2[1][1])
            p2_qproj(t0_, n_)
            mtoks = [("memn", kc) for kc in range(8)]
            for ch in range(8):
                slot = ch // 4
                w = wslot(slot, 8, 512)
                c0 = (ch % 4) * 128
                pt, ptok = next_ps()
                tk.mm(pt[:, 0:NMEM], [(w[:, kc, c0:c0 + 128], memn[:, kc, :]) for kc in range(8)],
                      R=[("W", slot)] + mtoks, W=[ptok])
                tk.op("act", lambda h, pt=pt, ch=ch: h.copy(out=KT[:, ch, :], in_=pt[:, 0:NMEM]), R=[ptok], W=[("KT", ch)])
            for mc in range(2):
                for nh in range(2):
                    slot = 2 + nh
                    w = wslot(slot, 8, 512)
                    pt, ptok = next_ps()
                    tk.mm(pt[:, :], [(memn[:, kc, mc * 128:(mc + 1) * 128], w[:, kc, :]) for kc in range(8)],
                          R=[("W", slot)] + mtoks, W=[ptok])
                    tk.op("act", lambda h, pt=pt, mc=mc, nh=nh: h.copy(out=Vm[:, mc, nh * 512:(nh + 1) * 512], in_=pt[:, :]),
                          R=[ptok], W=[("Vm", mc, nh)])
            tk.barrier(("pe", "act", "dve"))

            for i, (t0, n) in enumerate(tiles2):
                nxt = tiles2[i + 1] if i + 1 < len(tiles2) else None
                if i > 0:
                    if nxt is not None:
                        sq_part(xsrcs(nxt[0], nxt[1]), sq, "sq", nxt[1])
                    p2_qproj(t0, n)
                pend = []
                for hx in range(4):
                    eis = p2_S(hx, n)
                    pend.append((hx, eis))
                    if len(pend) == 2:
                        ph, pe_ = pend.pop(0)
                        p2_RPV(ph, n, pe_)
                for ph, pe_ in pend:
                    p2_RPV(ph, n, pe_)
                if nxt is not None:
                    rstd_part(8, ones_d[:], "ones_d", nxt[1], sq, "sq", lnb, "lnb", rstd, "rstd")
                    h_part(nxt[0], nxt[1], cb + O_X, rstd, hb, htoks)
                proj_residual((6, 7), onb, [("onb", ii) for ii in range(8)], t0, n)

            tk.barrier(COMPUTE)

            def load_e(e):
                load_w((2 * e) % 8, w_up_d[l][:, :, e * 512:(e + 1) * 512], 8, 512)
                load_w((2 * e + 1) % 8, w_down_d[l][:, 4 * e:4 * e + 4, :], 4, 1024)

            for e in range(4):
                load_e(e)

            cv = Carve()
            A = cv.get([8, T], BF)
            N3 = 256
            sq = cv.get([8, N3], BF)
            lnb = cv.get([1, N3], F32)[:, 0, :]
            rstd = cv.get([1, N3], F32)[:, 0, :]
            hid = cv.get([2, 4, 512], BF)
            ostk = [0]

            def final_tile(i3):
                s0, sn = tiles3[i3]
                for t0 in range(s0, s0 + sn, N3):
                    rms_rstd(xsrcs(t0, N3), ones_d[:], "ones_d", N3, sq, lnb, rstd, "rstd")
                    for kc in range(8):
                        tk.op("dve", lambda h, kc=kc, t0=t0: h.scalar_tensor_tensor(
                            out=xT[:, kc, t0:t0 + N3], in0=xT[:, kc, t0:t0 + N3], scalar=col(C_FIN + kc),
                            in1=rstd[:, 0:N3], op0=ALU.mult, op1=ALU.mult),
                            R=["rstd", "cst"], W=xtoks(kc, t0, N3))
                for kc in range(8):
                    oi = ostk[0] % 3
                    ostk[0] += 1
                    tk.dma("sp", out_d[kc * 128:(kc + 1) * 128, s0 - HALO:s0 - HALO + sn], xT[:, kc, s0:s0 + sn],
                           f"o{oi}", R=xtoks(kc, s0, sn))

            tiles3 = list(tiles2)
            tiles_a = []
            for (s0, sn) in tiles3:
                tcur = s0
                while tcur < s0 + sn:
                    nn = min(N3, s0 + sn - tcur)
                    tiles_a.append((tcur, nn))
                    tcur += nn
            def a_norm(i3):
                if i3 >= len(tiles3):
                    return
                s0, sn = tiles3[i3]
                for (t0, n) in tiles_a:
                    if t0 >= s0 and t0 < s0 + sn:
                        norm_h(t0, n, cb + O_MLP, sq, lnb, rstd, A[:, :, t0:t0 + n],
                               [("A", kc, t0 // 128) for kc in range(8)])


            def a_toks(t0, n):
                ks = set()
                for (s0, sn) in tiles_a:
                    if s0 < t0 + n and s0 + sn > t0:
                        ks.add(s0 // 128)
                return [("A", kc, j) for kc in range(8) for j in sorted(ks)]

            def up(e, i):
                t0, n = tiles3[i]
                hs = (e * len(tiles3) + i) % 2
                wu = wslot((2 * e) % 8, 8, 512)
                for f in range(4):
                    pt, ptok = next_ps()
                    tk.mm(pt[:, 0:n], [(wu[:, kc, f * 128:(f + 1) * 128], A[:, kc, t0:t0 + n]) for kc in range(8)],
                          R=[("W", (2 * e) % 8)] + a_toks(t0, n), W=[ptok])
                    tk.op("act", lambda h, pt=pt, f=f, hs=hs, n=n: h.activation(out=hid[:, hs, f, 0:n], in_=pt[:, 0:n],
                                                                             func=AF.Relu), R=[ptok], W=[("hid", hs, f)])
                    tk.op("dve", lambda h, f=f, hs=hs, n=n: h.tensor_tensor(out=hid[:, hs, f, 0:n], in0=hid[:, hs, f, 0:n],
                                                                         in1=hid[:, hs, f, 0:n], op=ALU.mult),
                          W=[("hid", hs, f)])

            def down(e, i):
                t0, n = tiles3[i]
                hs = (e * len(tiles3) + i) % 2
                slot = (2 * e + 1) % 8
                wd = wslot(slot, 4, 1024)
                for oc in range(8):
                    pt, ptok = next_ps()
                    tk.mm(pt[:, 0:n], [(wd[:, f, oc * 128:(oc + 1) * 128], hid[:, hs, f, 0:n]) for f in range(4)],
                          R=[("W", slot)] + [("hid", hs, f) for f in range(4)], W=[ptok])
                    tk.op("dve", lambda h, oc=oc, pt=pt, t0=t0, n=n: h.tensor_tensor(
                        out=xT[:, oc, t0:t0 + n], in0=pt[:, 0:n], in1=xT[:, oc, t0:t0 + n], op=ALU.add),
                        R=[ptok], W=xtoks(oc, t0, n))

            steps = [(e, i) for e in range(8) for i in range(len(tiles3))]
            last_layer = (l == DEPTH - 1)

            def halves(i3):
                if i3 < 0 or i3 >= len(tiles3):
                    return []
                s0, sn = tiles3[i3]
                return [(t0, n) for (t0, n) in tiles_a if s0 <= t0 < s0 + sn]

            def nsq(hv):
                sq_part(xsrcs(hv[0], hv[1]), sq, "sq", hv[1])

            def nrest_a(hv):
                t0, n = hv
                rstd_part(8, ones_d[:], "ones_d", n, sq, "sq", lnb, "lnb", rstd, "rstd")
                h_part(t0, n, cb + O_MLP, rstd, A[:, :, t0:t0 + n], [("A", kc, t0 // 128) for kc in range(8)])

            def nrest_f(hv):
                t0, n = hv
                rstd_part(8, ones_d[:], "ones_d", n, sq, "sq", lnb, "lnb", rstd, "rstd")
                for kc in range(8):
                    tk.op("dve", lambda h, kc=kc: h.scalar_tensor_tensor(
                        out=xT[:, kc, t0:t0 + n], in0=xT[:, kc, t0:t0 + n], scalar=col(C_FIN + kc),
                        in1=rstd[:, 0:n], op0=ALU.mult, op1=ALU.mult),
                        R=["rstd", "cst"], W=xtoks(kc, t0, n))

            def out_dma(i3):
                s0, sn = tiles3[i3]
                for kc in range(8):
                    oi = ostk[0] % 3
                    ostk[0] += 1
                    tk.dma("sp", out_d[kc * 128:(kc + 1) * 128, s0 - HALO:s0 - HALO + sn], xT[:, kc, s0:s0 + sn],
                           f"o{oi}", R=xtoks(kc, s0, sn))

            a_norm(0)
            a_norm(1)
            up(*steps[0])
            for si, (e, i) in enumerate(steps):
                H, nrest = [], None
                if e == 0:
                    H, nrest = halves(i + 2), nrest_a
                elif last_layer and e == 7:
                    H, nrest = halves(i - 1), nrest_f
                if H:
                    nsq(H[0])
                if si + 1 < len(steps):
                    up(*steps[si + 1])
                if H:
                    nrest(H[0])
                    if len(H) > 1:
                        nsq(H[1])
                down(e, i)
                if len(H) > 1:
                    nrest(H[1])
                if H and nrest is nrest_f:
                    out_dma(i - 1)
                if i == len(tiles3) - 1 and e + 4 < 8:
                    load_e(e + 4)
            if last_layer:
                for hv in halves(len(tiles3) - 1):
                    nsq(hv)
                    nrest_f(hv)
                out_dma(len(tiles3) - 1)

            tk.barrier(COMPUTE)

        sp = tk.eng["sp"]
        for i in range(3):
            d = tk.dsem[f"o{i}"]
            if d[2] > 0:
                sp.h.wait_ge(d[0], d[2])
    return nc


def _prep_shared(norm_mix_g, w_in, conv_w, sinks, gnorm_conv_g, gnorm_attn_g, w_out, norm_x_g, norm_mem_g,
                 wx_q, wx_kv, wx_o, norm_mlp_g, w_up, w_down, final_g):
    f32 = np.float32
    qbase, kbase, vbase = 1536, 2048, 2176
    head_pair = []
    for j in range(4):
        head_pair += list(range(j * 64, j * 64 + 64)) + list(range((4 + j) * 64, (4 + j) * 64 + 64))
    head_pair = np.array(head_pair)

    def swap_halves(cols):
        cols = np.asarray(cols).reshape(-1, 2, 32)
        return cols[:, ::-1, :].reshape(-1)

    qcols = qbase + head_pair
    kcols = kbase + np.arange(128)
    perm = np.concatenate([
        np.arange(512, 1024), np.arange(1024, 1536), np.arange(0, 512),
        qcols, kcols, vbase + np.arange(128)])
    assert perm.shape[0] == WIN_COLS

    def kmajor(w, nk):
        return np.ascontiguousarray(w.reshape(nk, 128, w.shape[1]).transpose(1, 0, 2))

    rows_out = np.concatenate([np.arange(512), 512 + head_pair])
    sh = {}
    sh["w_in"] = np.stack([kmajor(np.asarray(w_in[l], f32)[:, perm], 8) for l in range(DEPTH)])
    sh["w_out"] = np.stack([kmajor(np.asarray(w_out[l], f32)[rows_out, :], 8) for l in range(DEPTH)])
    sh["wx_q"] = np.stack([kmajor(np.asarray(wx_q[l], f32), 8) for l in range(DEPTH)])
    sh["wx_kv"] = np.stack([kmajor(np.asarray(wx_kv[l], f32), 8) for l in range(DEPTH)])
    sh["wx_o"] = np.stack([kmajor(np.asarray(wx_o[l], f32), 8) for l in range(DEPTH)])
    sh["w_up"] = np.stack([kmajor(np.asarray(w_up[l], f32), 8) for l in range(DEPTH)])
    sh["w_down"] = np.stack([kmajor(np.asarray(w_down[l], f32), 32) for l in range(DEPTH)])
    sh["sinks"] = np.ascontiguousarray(np.asarray(sinks, f32).reshape(1, 16))

    cst = np.zeros((128, NCST), f32)
    cst[:, C_EPS] = EPS
    cst[:, C_HPI] = math.pi / 2
    p = np.arange(128)
    jj = (p % 64) % 32
    cst[:, C_INVF] = (10000.0 ** (-(2.0 * jj) / 64.0)).astype(f32)
    cst[:, C_SIGN] = np.where((p % 64) < 32, -1.0, 1.0)

    def cols8(v):
        return np.asarray(v, f32).reshape(-1, 128).T

    for l in range(DEPTH):
        cb = C_L0 + l * C_LSTRIDE
        cst[:, cb + O_MIX:cb + O_MIX + 8] = cols8(norm_mix_g[l])
        cst[:, cb + O_X:cb + O_X + 8] = cols8(norm_x_g[l])
        cst[:, cb + O_MEM:cb + O_MEM + 8] = cols8(norm_mem_g[l])
        cst[:, cb + O_MLP:cb + O_MLP + 8] = cols8(norm_mlp_g[l])
        cst[:, cb + O_GC:cb + O_GC + 4] = cols8(gnorm_conv_g[l])
        cst[:, cb + O_GA:cb + O_GA + 4] = cols8(np.asarray(gnorm_attn_g[l], f32)[head_pair])
        for k in range(3):
            cst[:, cb + O_CW + k * 4:cb + O_CW + k * 4 + 4] = cols8(conv_w[l][k])
    cst[:, C_FIN:C_FIN + 8] = cols8(final_g)
    sh["cst"] = cst
    return sh


_NC_CACHE = {}


def kernel(x, mem, positions, norm_mix_g, w_in, conv_w, sinks, gnorm_conv_g, gnorm_attn_g, w_out, norm_x_g,
           norm_mem_g, wx_q, wx_kv, wx_o, norm_mlp_g, w_up, w_down, final_g):
    f32 = np.float32
    x = np.asarray(x, f32)
    mem = np.asarray(mem, f32)
    positions = np.asarray(positions, np.int32)
    sh = _prep_shared(norm_mix_g, w_in, conv_w, sinks, gnorm_conv_g, gnorm_attn_g, w_out, norm_x_g, norm_mem_g,
                      wx_q, wx_kv, wx_o, norm_mlp_g, w_up, w_down, final_g)
    kk = np.arange(128)[:, None]
    qq = np.arange(128)[None, :]
    m_prev = (kk > qq).astype(f32)
    m_cur = (kk <= qq).astype(f32)
    partner = np.where((np.arange(128) % 64) < 32, np.arange(128) + 32, np.arange(128) - 32)
    pm = np.zeros((128, 128), f32)
    pm[partner, np.arange(128)] = 1.0
    in_maps = []
    for c in range(NCORES):
        b, hh = c // 2, c % 2
        xs = np.zeros((T, D), f32)
        ps = np.zeros((1, T), np.int32)
        if hh == 0:
            xs[HALO:] = x[b, 0:TOK]
            ps[0, HALO:] = positions[b, 0:TOK]
        else:
            xs[:] = x[b, TOK - HALO:SEQ]
            ps[0, :] = positions[b, TOK - HALO:SEQ]
        m_first = m_prev if hh == 1 else np.zeros_like(m_prev)
        masks = np.concatenate([np.stack([np.tile(m, (1, 4)) for m in (m_prev, m_cur, m_first)], axis=1).reshape(128, 3 * 512), np.eye(128, dtype=f32), pm], axis=1)
        cst = sh["cst"].copy()
        cst[:, C_FLAG] = float(hh)
        d = {
            "xT": np.ascontiguousarray(xs.T),
            "pos": ps,
            "memT": np.ascontiguousarray(mem[b].T),
            "cst": cst,
            "sinks": sh["sinks"],
            "masks": np.ascontiguousarray(masks),
        }
        for k in ("w_in", "w_out", "wx_q", "wx_kv", "wx_o", "w_up", "w_down"):
            d[k] = sh[k]
        in_maps.append(d)
    if "nc" not in _NC_CACHE:
        _NC_CACHE["nc"] = build_program()
    nc = _NC_CACHE["nc"]
    res = run_bass_kernel_spmd(nc, in_maps, core_ids=list(range(NCORES)))
    out = np.empty((4, SEQ, D), f32)
    for c in range(NCORES):
        b, hh = c // 2, c % 2
        out[b, hh * TOK:(hh + 1) * TOK, :] = np.asarray(res.results[c]["outT"]).T
    return out
```

```python
import math
from contextlib import ExitStack

import numpy as np
import concourse.bass as bass
import concourse.mybir as mybir
from concourse.bass_utils import run_bass_kernel_spmd

F32 = mybir.dt.float32
BF = mybir.dt.bfloat16
I32 = mybir.dt.int32
AF = mybir.ActivationFunctionType
ALU = mybir.AluOpType

NCORES = 8
DEPTH = 2
D = 1024
SEQ = 4096
TOK = 2048
HALO = 256
T = TOK + HALO
NB = T // 128
NMEM = 256
DFF = 4096
EPS = 1e-6
TWO_PI = 2.0 * math.pi

C_EPS, C_HPI, C_INVF, C_SIGN, C_FLAG = 0, 1, 2, 3, 4
C_L0 = 8
C_LSTRIDE = 52
O_MIX, O_X, O_MEM, O_MLP, O_GC, O_GA, O_CW = 0, 8, 16, 24, 32, 36, 40
C_FIN = C_L0 + DEPTH * C_LSTRIDE
NCST = C_FIN + 8

WIN_COLS = 2304


class _Ev:
    __slots__ = ("sem", "key", "val", "eng", "clock")

    def __init__(self, sem, key, val, eng, clock=None):
        self.sem, self.key, self.val, self.eng = sem, key, val, eng
        self.clock = clock


class _Eng:
    def __init__(self, name, h, sem, key):
        self.name, self.h, self.sem, self.key = name, h, sem, key
        self.n = 0
        self.seen = {}
        self.last_clock = None


class Trk:
    def __init__(self, nc, es):
        self.nc, self.es = nc, es
        self.tok = {}
        self.eng = {}
        self.dsem = {}
        self.nsem = 0

    def _newsem(self, name):
        self.nsem += 1
        return self.es.enter_context(self.nc.semaphore(name)), self.nsem

    def add_engine(self, name, h):
        sem, key = self._newsem("e_" + name)
        self.eng[name] = _Eng(name, h, sem, key)

    def add_dsem(self, name):
        sem, key = self._newsem("d_" + name)
        self.dsem[name] = [sem, key, 0]

    def _st(self, t):
        s = self.tok.get(t)
        if s is None:
            s = [None, {}]
            self.tok[t] = s
        return s

    def _wait(self, e, ev):
        if ev is None:
            return
        if ev.eng == e.name and e.name == "pe":
            return
        if e.seen.get(ev.key, 0) >= ev.val:
            return
        e.h.wait_ge(ev.sem, ev.val)
        e.seen[ev.key] = ev.val
        if ev.clock:
            sn = e.seen
            for k, v in ev.clock.items():
                if sn.get(k, 0) < v:
                    sn[k] = v

    def _deps(self, e, R, W):
        for t in R:
            self._wait(e, self._st(t)[0])
        for t in W:
            st = self._st(t)
            self._wait(e, st[0])
            for ev in st[1].values():
                self._wait(e, ev)

    def _done(self, ev, R, W, rkey):
        for t in R:
            self._st(t)[1][rkey] = ev
        for t in W:
            st = self._st(t)
            st[0] = ev
            st[1] = {}

    def op(self, en, fn, R=(), W=()):
        e = self.eng[en]
        psr = [t for t in R if isinstance(t, tuple) and t[0] == "ps"]
        if psr:
            R = [t for t in R if not (isinstance(t, tuple) and t[0] == "ps")]
            W = list(W) + psr
        self._deps(e, R, W)
        ins = fn(e.h)
        e.n += 1
        ins.then_inc(e.sem, 1)
        e.last_clock = dict(e.seen)
        self._done(_Ev(e.sem, e.key, e.n, en, e.last_clock), R, W, en)

    def mm(self, out_ap, pairs, R=(), W=()):
        e = self.eng["pe"]
        self._deps(e, R, W)
        n = len(pairs)
        ins = None
        for i, (l, r) in enumerate(pairs):
            ins = e.h.matmul(out_ap, l, r, start=(i == 0), stop=(i == n - 1))
        e.n += 1
        ins.then_inc(e.sem, 1)
        e.last_clock = dict(e.seen)
        self._done(_Ev(e.sem, e.key, e.n, "pe", e.last_clock), R, W, "pe")

    def dma(self, qn, out, in_, sem, R=(), W=()):
        e = self.eng[qn]
        self._deps(e, R, W)
        d = self.dsem[sem]
        ins = e.h.dma_start(out=out, in_=in_)
        d[2] += 16
        ins.then_inc(d[0], 16)
        self._done(_Ev(d[0], d[1], d[2], "dma"), R, W, ("dma", sem))

    def finalize_sem(self, sem):
        d = self.dsem[sem]
        for st in self.tok.values():
            if st[0] is not None and st[0].key == d[1]:
                st[0] = _Ev(d[0], d[1], d[2], "dma")
            for k, ev in list(st[1].items()):
                if ev.key == d[1]:
                    st[1][k] = _Ev(d[0], d[1], d[2], "dma")

    def barrier(self, names):
        evs = []
        for n in names:
            e = self.eng[n]
            if e.n > 0:
                evs.append(_Ev(e.sem, e.key, e.n, n, e.last_clock))
        for n in names:
            e = self.eng[n]
            for ev in evs:
                if ev.eng != n:
                    self._wait(e, ev)
                elif n != "pe":
                    self._wait(e, ev)


def build_program():
    nc = bass.Bass("TRN2", target_bir_lowering=False)

    def din(name, shape, dt):
        return nc.dram_tensor(name, shape, dt, kind="ExternalInput").ap()

    xT_d = din("xT", [D, T], F32)
    pos_d = din("pos", [1, T], I32)
    memT_d = din("memT", [D, NMEM], F32)
    cst_d = din("cst", [128, NCST], F32)
    sinks_d = din("sinks", [1, 16], F32)
    masks_d = din("masks", [128, 3 * 512 + 256], F32)
    w_in_d = din("w_in", [DEPTH, 128, 8, WIN_COLS], F32)
    w_out_d = din("w_out", [DEPTH, 128, 8, D], F32)
    wx_q_d = din("wx_q", [DEPTH, 128, 8, D], F32)
    wx_kv_d = din("wx_kv", [DEPTH, 128, 8, 2 * D], F32)
    wx_o_d = din("wx_o", [DEPTH, 128, 8, D], F32)
    w_up_d = din("w_up", [DEPTH, 128, 8, DFF], F32)
    w_down_d = din("w_down", [DEPTH, 128, 32, D], F32)
    out_d = nc.dram_tensor("outT", [D, TOK], F32, kind="ExternalOutput").ap()

    with ExitStack() as es:
        def sb(name, shape, dt):
            return es.enter_context(nc.sbuf_tensor(name, shape, dt))

        xT = sb("xT_sb", [128, 8, T], F32)
        cosT = sb("cosT", [128, T], BF)
        sinT = sb("sinT", [128, T], BF)
        mski = sb("mski", [128, 3 * 512 + 256], BF)
        msk = mski[:, 0:1536].rearrange("p (a b) -> p a b", a=3)
        ident = mski[:, 1536:1664]
        permM = mski[:, 1664:1792]
        qb = sb("qb", [128, 3, 256], BF)
        dg = sb("dg", [128, 12, 128], BF)
        cst = sb("cst_sb", [128, NCST], F32)
        sinkexp = sb("sinkexp", [128, 16], F32)
        ones_d = sb("ones_d", [128, 128], BF)
        ones_g = sb("ones_g", [128, 128], BF)
        Wr = sb("Wr", [128, 8, 4096], BF)
        SBYTES = 53248
        S = sb("S", [128, SBYTES // 2], BF)
        psb = [es.enter_context(nc.psum_tensor(f"ps{i}", [128, 512], F32)) for i in range(8)]

        tk = Trk(nc, es)
        tk.add_engine("pe", nc.tensor)
        tk.add_engine("act", nc.scalar)
        tk.add_engine("dve", nc.vector)
        tk.add_engine("pool", nc.gpsimd)
        tk.add_engine("sp", nc.sync)
        for i in range(8):
            tk.add_dsem(f"w{i}")
        tk.add_dsem("ld")
        tk.add_dsem("xa")
        tk.add_dsem("xb")
        tk.add_dsem("xc")
        tk.add_dsem("ldm")
        tk.add_dsem("mem")
        for i in range(3):
            tk.add_dsem(f"o{i}")

        COMPUTE = ("pe", "act", "dve", "sp")

        class Carve:
            def __init__(self):
                self.off = 0

            def get(self, shape, dt):
                esz = 4 if dt in (F32, I32) else 2
                n = int(np.prod(shape))
                nb = n * esz
                a = self.off
                self.off += (nb + 63) // 64 * 64
                assert self.off <= SBYTES, (self.off, SBYTES)
                ap = S[:, a // 2:(a + nb) // 2]
                if dt != BF:
                    ap = ap.bitcast(dt)
                if len(shape) == 2:
                    return ap.rearrange("p (a b) -> p a b", a=shape[0])
                if len(shape) == 3:
                    return ap.rearrange("p (a b c) -> p a b c", a=shape[0], b=shape[1])
                return ap

        psctr = [0]

        ps_held = set()

        def next_ps(hold=False):
            while True:
                i = psctr[0] % 8
                psctr[0] += 1
                if i not in ps_held:
                    break
            if hold:
                ps_held.add(i)
            return psb[i], ("ps", i)

        def ps_release(ptok):
            ps_held.discard(ptok[1])

        def col(c):
            return cst[:, c:c + 1]

        def xtoks(kc, t0, n):
            return [("x", kc, j) for j in range(t0 // 256, (t0 + n + 255) // 256)]

        def wslot(slot, kcn, cols):
            return Wr[:, slot, 0:kcn * cols].rearrange("p (k c) -> p k c", k=kcn)

        def load_w(slot, src_ap, kcn, cols):
            dst = wslot(slot, kcn, cols)
            h = kcn // 2
            tk.dma("pool", dst[:, 0:h, :], src_ap[:, 0:h, :], f"w{slot}", W=[("W", slot)])
            tk.dma("pool", dst[:, h:kcn, :], src_ap[:, h:kcn, :], f"w{slot}", W=[("W", slot)])

        tk.dma("sp", cst[:], cst_d[:], "ld", W=["cst"])
        cv = Carve()
        posi = cv.get([T], I32)
        angf = cv.get([T], F32)
        tmpf = cv.get([T], F32)
        tmpi = cv.get([T], I32)
        rr = cv.get([T], F32)
        sinks_b = cv.get([16], F32)
        maskf = None
        tk.dma("sp", posi, pos_d.partition_broadcast(128), "ld", W=["posi"])
        tk.dma("sp", sinks_b, sinks_d.partition_broadcast(128), "ld", W=["sinks_b"])
        tk.finalize_sem("ld")
        for kc in range(8):
            tk.dma("sp", xT[:, kc, 0:256], xT_d[kc * 128:(kc + 1) * 128, 0:256], "xa", W=[("x", kc, 0)])
        tk.finalize_sem("xa")
        tk.dma("pool", mski[:], masks_d[:], "ldm", W=["msk"])

        tk.op("dve", lambda h: h.memset(ones_d[:], 1.0 / D), W=["ones_d"])
        tk.op("dve", lambda h: h.memset(ones_g[:], 1.0 / 512.0), W=["ones_g"])
        tk.op("dve", lambda h: h.tensor_copy(out=angf, in_=posi), R=["posi"], W=["angf"])
        tk.op("dve", lambda h: h.tensor_scalar(out=angf, in0=angf, scalar1=col(C_INVF), scalar2=None, op0=ALU.mult),
              R=["cst"], W=["angf"])
        for which in (0, 1):
            add = 0.0 if which == 0 else 0.25
            tk.op("dve", lambda h: h.tensor_scalar(out=tmpf, in0=angf, scalar1=1.0 / TWO_PI, scalar2=add,
                                                   op0=ALU.mult, op1=ALU.add), R=["angf"], W=["tmpf"])
            tk.op("dve", lambda h: h.tensor_copy(out=tmpi, in_=tmpf), R=["tmpf"], W=["tmpi"])
            tk.op("dve", lambda h: h.tensor_copy(out=tmpf, in_=tmpi), R=["tmpi"], W=["tmpf"])
            tk.op("dve", lambda h: h.scalar_tensor_tensor(out=rr, in0=tmpf, scalar=-TWO_PI, in1=angf,
                                                          op0=ALU.mult, op1=ALU.add), R=["tmpf", "angf"], W=["rr"])
            if which == 0:
                lo, hi = -3.14159, 3.14159
            else:
                lo, hi = -3.14159 - math.pi / 2, 3.14159 - math.pi / 2
            tk.op("dve", lambda h: h.tensor_scalar(out=rr, in0=rr, scalar1=lo, scalar2=hi, op0=ALU.max, op1=ALU.min),
                  W=["rr"])
            if which == 0:
                tk.op("act", lambda h: h.activation(out=sinT[:], in_=rr, func=AF.Sin, scale=col(C_SIGN)),
                      R=["rr", "cst"], W=["sinT"])
            else:
                tk.op("act", lambda h: h.activation(out=cosT[:], in_=rr, func=AF.Sin, bias=col(C_HPI), scale=1.0),
                      R=["rr", "cst"], W=["cosT"])
        tk.op("act", lambda h: h.activation(out=sinkexp[:], in_=sinks_b, func=AF.Exp), R=["sinks_b"], W=["sinkexp"])
        tk.barrier(COMPUTE)

        def sq_part(srcs, sq, sqn, n):
            for i, (ap, toks) in enumerate(srcs):
                tk.op("act", lambda h, ap=ap, i=i: h.activation(out=sq[:, i, 0:n], in_=ap, func=AF.Square),
                      R=toks, W=[(sqn, i)])

        def rstd_part(ns, ones_ap, ones_tok, n, sq, sqn, lnb, lnbn, rstd, rstd_tok):
            pt, ptok = next_ps()
            tk.mm(pt[:, 0:n], [(ones_ap, sq[:, i, 0:n]) for i in range(ns)],
                  R=[ones_tok] + [(sqn, i) for i in range(ns)], W=[ptok])
            tk.op("act", lambda h: h.activation(out=lnb[:, 0:n], in_=pt[:, 0:n], func=AF.Ln, bias=col(C_EPS), scale=1.0),
                  R=[ptok, "cst"], W=[lnbn])
            tk.op("act", lambda h: h.activation(out=rstd[:, 0:n], in_=lnb[:, 0:n], func=AF.Exp, scale=-0.5),
                  R=[lnbn], W=[rstd_tok])

        def rms_rstd(srcs, ones_ap, ones_tok, n, sq, lnb, rstd, rstd_tok, sqn="sq", lnbn="lnb"):
            sq_part(srcs, sq, sqn, n)
            rstd_part(len(srcs), ones_ap, ones_tok, n, sq, sqn, lnb, lnbn, rstd, rstd_tok)

        def xsrcs(t0, n):
            return [(xT[:, kc, t0:t0 + n], xtoks(kc, t0, n)) for kc in range(8)]

        def h_part(t0, n, gcol0, rstd, hdst, htoks, rstd_tok="rstd"):
            for kc in range(8):
                tk.op("dve", lambda h, kc=kc: h.scalar_tensor_tensor(
                    out=hdst[:, kc, 0:n], in0=xT[:, kc, t0:t0 + n], scalar=col(gcol0 + kc), in1=rstd[:, 0:n],
                    op0=ALU.mult, op1=ALU.mult), R=xtoks(kc, t0, n) + [rstd_tok, "cst"], W=[htoks[kc]])

        def norm_h(t0, n, gcol0, sq, lnb, rstd, hdst, htoks):
            rms_rstd(xsrcs(t0, n), ones_d[:], "ones_d", n, sq, lnb, rstd, "rstd")
            h_part(t0, n, gcol0, rstd, hdst, htoks)

        def proj_residual(wslots, hsrc, htoks, t0, n, nk=8):
            for oc in range(8):
                slot = wslots[oc // 4]
                w = wslot(slot, nk, 512)
                c0 = (oc % 4) * 128
                pt, ptok = next_ps()
                tk.mm(pt[:, 0:n], [(w[:, kc, c0:c0 + 128], hsrc[:, kc, 0:n]) for kc in range(nk)],
                      R=[("W", slot)] + list(htoks), W=[ptok])
                tk.op("dve", lambda h, oc=oc, pt=pt: h.tensor_tensor(
                    out=xT[:, oc, t0:t0 + n], in0=pt[:, 0:n], in1=xT[:, oc, t0:t0 + n], op=ALU.add),
                    R=[ptok], W=xtoks(oc, t0, n))

        for l in range(DEPTH):
            cb = C_L0 + l * C_LSTRIDE
            for s, (c0, nc_) in enumerate([(0, 512), (512, 512), (1024, 512), (1536, 512), (2048, 256)]):
                load_w(s, w_in_d[l][:, :, c0:c0 + nc_], 8, nc_)
            load_w(6, w_out_d[l][:, :, 0:512], 8, 512)
            load_w(7, w_out_d[l][:, :, 512:1024], 8, 512)
            if l == 0:
                tk._wait(tk.eng["sp"], tk._st(("W", 2))[0])
                for kc in range(8):
                    tk.dma("sp", xT[:, kc, 256:768], xT_d[kc * 128:(kc + 1) * 128, 256:768], "xb",
                           W=[("x", kc, 1), ("x", kc, 2)])
                tk.finalize_sem("xb")
                tk._wait(tk.eng["sp"], tk._st(("W", 7))[0])
                for kc in range(8):
                    tk.dma("sp", xT[:, kc, 768:T], xT_d[kc * 128:(kc + 1) * 128, 768:T], "xc",
                           W=[("x", kc, j) for j in range(3, T // 256)])
                tk.finalize_sem("xc")

            cv = Carve()
            N1 = 256
            sq = cv.get([8, N1], BF)
            sqg = cv.get([8, N1], BF)
            lnb = cv.get([1, N1], F32)[:, 0, :]
            lnb2 = cv.get([1, N1], F32)[:, 0, :]
            rstd = cv.get([1, N1], F32)[:, 0, :]
            hb = cv.get([8, N1], BF)
            gcs = cv.get([2, N1], F32)
            pbuf = cv.get([4, N1 + 4], BF)
            ybuf = cv.get([2, N1], F32)
            cvo = cv.get([4, N1], F32)
            rt = cv.get([4, N1], F32)
            qrot = cv.get([4, N1], BF)
            krot = cv.get([3, 128], BF)
            Vr = cv.get([3, 4, 64], BF)
            Eb = cv.get([4, 512], BF)
            den = cv.get([2, 512], F32)
            attn = cv.get([4, N1], F32)
            rstd2 = cv.get([2, N1], F32)
            mixed = cv.get([8, N1], BF)

            tk.op("dve", lambda h: h.memset(pbuf[:, :, 0:2], 0.0), W=[("p", c) for c in range(4)])
            for k in range(3):
                for c in range(4):
                    tk.op("dve", lambda h, k=k, c=c: h.tensor_scalar(
                        out=dg[:, k * 4 + c, :], in0=ident, scalar1=col(cb + O_CW + k * 4 + c), scalar2=None,
                        op0=ALU.mult), R=["msk", "cst"], W=[("dg", k * 4 + c)])
            tk.op("dve", lambda h: h.memset(krot[:], 0.0), W=[("krot", i) for i in range(3)])
            tk.op("dve", lambda h: h.memset(Vr[:, :, 0:4:3, :], 0.0), W=[("Vr", i) for i in range(3)])
            tk.op("dve", lambda h: h.memset(Vr[:, :, 1:3, :], 1.0), W=[("Vr", i) for i in range(3)])

            wgc, wxc, wgb = wslot(0, 8, 512), wslot(1, 8, 512), wslot(2, 8, 512)
            wq = wslot(3, 8, 512)
            wkv = wslot(4, 8, 256)
            htoks = [("h", kc) for kc in range(8)]

            if l == 0:
                tiles1 = [(i * N1, N1, False) for i in range(T // N1)]
            else:
                tiles1 = [(128, 128, True)] + [(i * N1, N1, False) for i in range(1, T // N1)]

            def inproj(w, slot, c0, n, hoff=0, hold=False):
                pt, ptok = next_ps(hold)
                tk.mm(pt[:, 0:n], [(w[:, kc, c0:c0 + 128], hb[:, kc, hoff:hoff + n]) for kc in range(8)],
                      R=[("W", slot)] + htoks, W=[ptok])
                return pt, ptok

            def p1_sq(td):
                t0, n, _ = td
                sq_part(xsrcs(t0, n), sq, "sq", n)

            def p1_normrest(td):
                t0, n, _ = td
                rstd_part(8, ones_d[:], "ones_d", n, sq, "sq", lnb, "lnb", rstd, "rstd")
                h_part(t0, n, cb + O_MIX, rstd, hb, htoks)

            gbps = {}

            def p1_conv_A(td, c, first):
                t0, n, kvonly = td
                g = c % 2
                pa, pat = inproj(wgc, 0, c * 128, n)
                tk.op("act", lambda h: h.copy(out=gcs[:, g, 0:n], in_=pa[:, 0:n]), R=[pat], W=[("gcs", g)])
                pbk, pbt = inproj(wxc, 1, c * 128, n)
                if first:
                    tk.op("dve", lambda h: h.scalar_tensor_tensor(
                        out=pbuf[:, c, 2:2 + n], in0=pbk[:, 0:n], scalar=col(C_FLAG), in1=gcs[:, g, 0:n],
                        op0=ALU.mult, op1=ALU.mult), R=[pbt, ("gcs", g), "cst"], W=[("p", c)])
                else:
                    tk.op("dve", lambda h: h.tensor_tensor(
                        out=pbuf[:, c, 2:2 + n], in0=pbk[:, 0:n], in1=gcs[:, g, 0:n], op=ALU.mult),
                        R=[pbt, ("gcs", g)], W=[("p", c)])
                if not kvonly:
                    gbps[c] = inproj(wgb, 2, c * 128, n, hold=True)

            def p1_conv_B(td, c):
                t0, n, kvonly = td
                g = c % 2
                if not kvonly:
                    py, pyt = next_ps()
                    tk.mm(py[:, 0:n], [(dg[:, k * 4 + c, :], pbuf[:, c, k:k + n]) for k in range(3)],
                          R=[("p", c)] + [("dg", k * 4 + c) for k in range(3)], W=[pyt])
                    tk.op("act", lambda h: h.copy(out=ybuf[:, g, 0:n], in_=py[:, 0:n]), R=[pyt], W=[("y", g)])
                tk.op("dve", lambda h: h.tensor_copy(out=pbuf[:, c, 0:2], in_=pbuf[:, c, n:n + 2]), W=[("p", c)])
                if not kvonly:
                    pg, pgt = gbps.pop(c)
                    tk.op("dve", lambda h: h.tensor_tensor(
                        out=cvo[:, c, 0:n], in0=pg[:, 0:n], in1=ybuf[:, g, 0:n], op=ALU.mult),
                        R=[pgt, ("y", g)], W=[("cvo", c)])
                    ps_release(pgt)

            def p1_conv_step(td, c, first=False):
                if c < 4:
                    p1_conv_A(td, c, first)
                if c >= 1:
                    p1_conv_B(td, c - 1)

            def rotary(pq, pqt, pqs, pqst, dst, dtok, r, poff, ncols, tt0):
                tk.op("dve", lambda h: h.tensor_tensor(out=rt[:, 2 * r, 0:ncols], in0=pq[:, poff:poff + ncols],
                                                       in1=cosT[:, tt0:tt0 + ncols], op=ALU.mult),
                      R=[pqt, "cosT"], W=[("rt", 2 * r)])
                tk.op("dve", lambda h: h.tensor_tensor(out=rt[:, 2 * r + 1, 0:ncols], in0=pqs[:, poff:poff + ncols],
                                                       in1=sinT[:, tt0:tt0 + ncols], op=ALU.mult),
                      R=[pqst, "sinT"], W=[("rt", 2 * r + 1)])
                tk.op("dve", lambda h: h.tensor_tensor(out=dst, in0=rt[:, 2 * r, 0:ncols],
                                                        in1=rt[:, 2 * r + 1, 0:ncols], op=ALU.add),
                      R=[("rt", 2 * r), ("rt", 2 * r + 1)], W=[dtok])

            def _rot_flush(item, n):
                pq, pqt, qi, outs = item
                pqs, pqst = next_ps()
                tk.mm(pqs[:, 0:n], [(permM, qb[:, qi, 0:n])], R=[("qb", qi), "msk"], W=[pqst])
                for (dst, dtok, r, poff, ncols, tt0) in outs:
                    rotary(pq, pqt, pqs, pqst, dst, dtok, r, poff, ncols, tt0)
                ps_release(pqt)

            def p1_q(td):
                t0, n, kvonly = td
                if kvonly:
                    return
                pend = []
                for j in range(4):
                    pq, pqt = inproj(wq, 3, j * 128, n, hold=True)
                    qi = j % 3
                    tk.op("act", lambda h: h.copy(out=qb[:, qi, 0:n], in_=pq[:, 0:n]), R=[pqt], W=[("qb", qi)])
                    pend.append((pq, pqt, qi, [(qrot[:, j, 0:n], ("qrot", j), j % 2, 0, n, t0)]))
                    if len(pend) == 3:
                        _rot_flush(pend.pop(0), n)
                while pend:
                    _rot_flush(pend.pop(0), n)

            def p1_kv(td):
                t0, n, kvonly = td
                pk, pkt = inproj(wkv, 4, 0, n, hold=True)
                tk.op("act", lambda h: h.copy(out=qb[:, 0, 0:n], in_=pk[:, 0:n]), R=[pkt], W=[("qb", 0)])
                kouts = []
                for b in range(n // 128):
                    B = (t0 + b * 128) // 128
                    ri = B % 3
                    kouts.append((krot[:, ri, :], ("krot", ri), b % 2, b * 128, 128, t0 + b * 128))
                for b in range(n // 128):
                    B = (t0 + b * 128) // 128
                    ri = B % 3
                    pv, pvt = next_ps()
                    tk.mm(pv[:, 0:128], [(hb[:, kc, b * 128:(b + 1) * 128], wkv[:, kc, 128:256]) for kc in range(8)],
                          R=[("W", 4)] + htoks, W=[pvt])
                    tk.op("act", lambda h, pv=pv, ri=ri: h.copy(
                        out=Vr[:, ri, 0:4:3, :], in_=pv[:, 0:128].rearrange("p (a b) -> p a b", a=2)),
                        R=[pvt], W=[("Vr", ri)])
                _rot_flush((pk, pkt, 0, kouts), n)

            def p1_S(td, b, g):
                t0, n, _ = td
                B = t0 // 128 + b
                rc, rp = B % 3, (B - 1) % 3
                orow = slice(64 * g, 64 * g + 64)
                qr = qrot[orow, :, b * 128:(b + 1) * 128]
                qtoks = [("qrot", j) for j in range(4)]
                ets = []
                for which, rix in ((0, rp), (1, rc)):
                    pt, ptok = next_ps()
                    tk.mm(pt[:, :], [(krot[orow, rix, :], qr)], R=[("krot", rix)] + qtoks, W=[ptok])
                    ei = 2 * g + which
                    tk.op("act", lambda h, pt=pt, ei=ei: h.activation(out=Eb[:, ei, :], in_=pt[:, :],
                                                                      func=AF.Exp, scale=0.125),
                          R=[ptok], W=[("E", ei)])
                    mi = 1 if which == 1 else (2 if B == 2 else 0)
                    tk.op("dve", lambda h, ei=ei, mi=mi: h.tensor_tensor(out=Eb[:, ei, :], in0=Eb[:, ei, :],
                                                                          in1=msk[:, mi, :], op=ALU.mult),
                          R=["msk"], W=[("E", ei)])
                    ets.append((ei, rix))
                return ets

            def p1_PV(td, b, g, ets):
                orow = slice(64 * g, 64 * g + 64)
                srow = slice(64 * (1 - g), 64 * (1 - g) + 64)
                po, pot = next_ps()
                tk.mm(po[:, :], [(Vr[:, rix, 2 * g:2 * g + 2, :].rearrange("p a b -> p (a b)"), Eb[:, ei, :])
                                 for (ei, rix) in ets],
                      R=[("E", ei) for ei, _ in ets] + [("Vr", rix) for _, rix in ets], W=[pot])
                sk = sinkexp[srow, l * 8 + 4 * g:l * 8 + 4 * g + 4].unsqueeze(2).to_broadcast([64, 4, 128])
                tk.op("dve", lambda h: h.tensor_tensor(
                    out=den[orow, b, :].rearrange("p (a b) -> p a b", a=4),
                    in0=po[srow, :].rearrange("p (a b) -> p a b", a=4), in1=sk, op=ALU.add),
                    R=[pot, "sinkexp"], W=[("den", b, g)])
                tk.op("act", lambda h: h.copy(out=attn[orow, :, b * 128:(b + 1) * 128],
                                              in_=po[orow, :].rearrange("p (a b) -> p a b", a=4)),
                      R=[pot], W=[("attn", g, b)])

            def p1_attn_norm(td):
                for b in range(2):
                    dt_ = [("den", b, 0), ("den", b, 1)]
                    tk.op("act", lambda h: h.activation(out=den[:, b, :], in_=den[:, b, :], func=AF.Ln), W=dt_)
                for b in range(2):
                    dt_ = [("den", b, 0), ("den", b, 1)]
                    tk.op("act", lambda h: h.activation(out=den[:, b, :], in_=den[:, b, :], func=AF.Exp, scale=-1.0), W=dt_)
                for b in range(2):
                    dt_ = [("den", b, 0), ("den", b, 1)]
                    tk.op("dve", lambda h: h.tensor_tensor(
                        out=attn[:, :, b * 128:(b + 1) * 128], in0=attn[:, :, b * 128:(b + 1) * 128],
                        in1=den[:, b, :].rearrange("p (a b) -> p a b", a=4), op=ALU.mult),
                        R=dt_, W=[("attn", 0, b), ("attn", 1, b)])

            def p1_attn(td, nx):
                u = [(b, g) for b in range(2) for g in range(2)]

                def cstep(c):
                    if nx is not None:
                        p1_conv_step(nx, c)

                e0 = p1_S(td, *u[0])
                e1 = p1_S(td, *u[1])
                cstep(0)
                p1_PV(td, u[0][0], u[0][1], e0)
                e2 = p1_S(td, *u[2])
                p1_gconv_rest(td)
                cstep(1)
                p1_PV(td, u[1][0], u[1][1], e1)
                e3 = p1_S(td, *u[3])
                cstep(2)
                p1_PV(td, u[2][0], u[2][1], e2)
                p1_PV(td, u[3][0], u[3][1], e3)
                cstep(3)
                p1_attn_norm(td)
                cstep(4)

            atoks_all = [("attn", g, b) for g in range(2) for b in range(2)]

            def p1_gconv_sq(td):
                n = td[1]
                sq_part([(cvo[:, c, 0:n], [("cvo", c)]) for c in range(4)], sqg, "sqg", n)

            def p1_gconv_rest(td):
                n = td[1]
                rstd_part(4, ones_g[:], "ones_g", n, sqg, "sqg", lnb2, "lnb2", rstd2[:, 0, :], ("rstd2", 0))
                for c in range(4):
                    tk.op("dve", lambda h, c=c: h.scalar_tensor_tensor(
                        out=mixed[:, c, 0:n], in0=cvo[:, c, 0:n], scalar=col(cb + O_GC + c), in1=rstd2[:, 0, 0:n],
                        op0=ALU.mult, op1=ALU.mult), R=[("cvo", c), ("rstd2", 0), "cst"], W=[("mixed", c)])

            def p1_gattn_sq(td):
                n = td[1]
                for j in range(4):
                    tk.op("act", lambda h, j=j: h.activation(out=sqg[:, 4 + j, 0:n], in_=attn[:, j, 0:n], func=AF.Square),
                          R=atoks_all, W=[("sqg", 4 + j)])

            def p1_gattn(td):
                n = td[1]
                pt, ptok = next_ps()
                tk.mm(pt[:, 0:n], [(ones_g[:], sqg[:, 4 + j, 0:n]) for j in range(4)],
                      R=["ones_g"] + [("sqg", 4 + j) for j in range(4)], W=[ptok])
                tk.op("act", lambda h: h.activation(out=lnb2[:, 0:n], in_=pt[:, 0:n], func=AF.Ln, bias=col(C_EPS), scale=1.0),
                      R=[ptok, "cst"], W=["lnb2"])
                tk.op("act", lambda h: h.activation(out=rstd2[:, 1, 0:n], in_=lnb2[:, 0:n], func=AF.Exp, scale=-0.5),
                      R=["lnb2"], W=[("rstd2", 1)])
                for j in range(4):
                    tk.op("dve", lambda h, j=j: h.scalar_tensor_tensor(
                        out=mixed[:, 4 + j, 0:n], in0=attn[:, j, 0:n], scalar=col(cb + O_GA + j), in1=rstd2[:, 1, 0:n],
                        op0=ALU.mult, op1=ALU.mult), R=atoks_all + [("rstd2", 1), "cst"], W=[("mixed", 4 + j)])

            def tl(i):
                return tiles1[i] if i < len(tiles1) else None

            p1_sq(tl(0))
            p1_normrest(tl(0))
            p1_sq(tl(1))
            for c in range(5):
                p1_conv_step(tl(0), c, True)
            p1_q(tl(0))
            p1_kv(tl(0))
            p1_normrest(tl(1))
            p1_sq(tl(2))
            if not tl(0)[2]:
                p1_gconv_sq(tl(0))
            for i, td in enumerate(tiles1):
                nx, nx2, nx3 = tl(i + 1), tl(i + 2), tl(i + 3)
                kvonly = td[2]
                if not kvonly:
                    p1_attn(td, nx)
                elif nx is not None:
                    for c in range(5):
                        p1_conv_step(nx, c)
                if nx is not None:
                    p1_q(nx)
                if not kvonly:
                    p1_gattn_sq(td)
                    p1_gattn(td)
                if nx is not None:
                    p1_kv(nx)
                if nx2 is not None:
                    p1_normrest(nx2)
                if nx3 is not None:
                    p1_sq(nx3)
                if nx is not None:
                    p1_gconv_sq(nx)
                if not kvonly:
                    proj_residual((6, 7), mixed, [("mixed", ii) for ii in range(8)], td[0], td[1])

            tk.barrier(COMPUTE)

            load_w(5, wx_q_d[l][:, :, 512:1024], 8, 512)
            load_w(4, wx_q_d[l][:, :, 0:512], 8, 512)
            for s in range(4):
                load_w(s, wx_kv_d[l][:, :, s * 512:(s + 1) * 512], 8, 512)
            load_w(6, wx_o_d[l][:, :, 0:512], 8, 512)
            load_w(7, wx_o_d[l][:, :, 512:1024], 8, 512)

            cv = Carve()
            N2 = 512
            KT = cv.get([8, NMEM], BF)
            Vm = cv.get([2, D], BF)
            sq = cv.get([8, N2], BF)
            lnb = cv.get([1, N2], F32)[:, 0, :]
            rstd = cv.get([1, N2], F32)[:, 0, :]
            hb = cv.get([8, N2], BF)
            qT = cv.get([8, N2], BF)
            cv_mark = cv.off
            memT = cv.get([8, NMEM], F32)
            memn = cv.get([8, NMEM], BF)
            lnb_m = cv.get([1, NMEM], F32)[:, 0, :]
            rstd_m = cv.get([1, NMEM], F32)[:, 0, :]
            cv.off = cv_mark
            E2 = cv.get([4, N2], BF)
            rec = cv.get([2, N2], F32)
            onb = cv.get([8, N2], BF)
            htoks = [("h", kc) for kc in range(8)]
            tstart = 128 if l == 0 else 256
            if l == 0:
                tiles2 = [(128, 448), (576, 448), (1024, 448), (1472, 448), (1920, 384)]
            else:
                tiles2 = [(256 + 512 * i, 512) for i in range(4)]

            def p2_qproj(t0, n):
                for ch in range(8):
                    slot = 4 + ch // 4
                    w = wslot(slot, 8, 512)
                    c0 = (ch % 4) * 128
                    pt, ptok = next_ps()
                    tk.mm(pt[:, 0:n], [(w[:, kc, c0:c0 + 128], hb[:, kc, 0:n]) for kc in range(8)],
                          R=[("W", slot)] + htoks, W=[ptok])
                    tk.op("act", lambda h, pt=pt, ch=ch: h.copy(out=qT[:, ch, 0:n], in_=pt[:, 0:n]), R=[ptok], W=[("qT", ch)])

            def p2_S(hx, n):
                eis = []
                for mc in range(2):
                    pt, ptok = next_ps()
                    tk.mm(pt[:, 0:n], [(KT[:, hx * 2 + dc, mc * 128:(mc + 1) * 128], qT[:, hx * 2 + dc, 0:n])
                                       for dc in range(2)],
                          R=[("KT", hx * 2), ("KT", hx * 2 + 1), ("qT", hx * 2), ("qT", hx * 2 + 1)], W=[ptok])
                    ei = (hx % 2) * 2 + mc
                    tk.op("act", lambda h, pt=pt, ei=ei: h.activation(out=E2[:, ei, 0:n], in_=pt[:, 0:n], func=AF.Exp,
                                                                      scale=1.0 / 16.0), R=[ptok], W=[("E2", ei)])
                    eis.append(ei)
                return eis

            def p2_RPV(hx, n, eis):
                etoks = [("E2", ei) for ei in eis]
                pr, prt = next_ps()
                tk.mm(pr[:, 0:n], [(ones_d[:], E2[:, ei, 0:n]) for ei in eis], R=etoks + ["ones_d"], W=[prt])
                ri = hx % 2
                tk.op("act", lambda h: h.activation(out=rec[:, ri, 0:n], in_=pr[:, 0:n], func=AF.Ln, scale=float(D)),
                      R=[prt], W=[("rec", ri)])
                tk.op("act", lambda h: h.activation(out=rec[:, ri, 0:n], in_=rec[:, ri, 0:n], func=AF.Exp, scale=-1.0),
                      W=[("rec", ri)])
                for dc in range(2):
                    po, pot = next_ps()
                    tk.mm(po[:, 0:n], [(Vm[:, mc, hx * 256 + dc * 128: hx * 256 + (dc + 1) * 128], E2[:, eis[mc], 0:n])
                                       for mc in range(2)],
                          R=etoks + [("Vm", mc, nh) for mc in range(2) for nh in range(2)], W=[pot])
                    tk.op("dve", lambda h, po=po, dc=dc: h.tensor_tensor(
                        out=onb[:, hx * 2 + dc, 0:n], in0=po[:, 0:n], in1=rec[:, ri, 0:n], op=ALU.mult),
                        R=[pot, ("rec", ri)], W=[("onb", hx * 2 + dc)])

            for kc in range(8):
                tk.dma("sp", memT[:, kc, :], memT_d[kc * 128:(kc + 1) * 128, :], "mem", W=[("memT", kc)])
            tk.finalize_sem("mem")
            t0_, n_ = tiles2[0]
            norm_h(t0_, n_, cb + O_X, sq, lnb, rstd, hb, htoks)
            sq_part([(memT[:, kc, :], [("memT", kc)]) for kc in range(8)], sq, "sq", NMEM)
            rstd_part(8, ones_d[:], "ones_d", NMEM, sq, "sq", lnb_m, "lnb_m", rstd_m, "rstd_m")
            for kc in range(8):
                tk.op("dve", lambda h, kc=kc: h.scalar_tensor_tensor(
                    out=memn[:, kc, :], in0=memT[:, kc, :], scalar=col(cb + O_MEM + kc), in1=rstd_m[:, 0:NMEM],
                    op0=ALU.mult, op1=ALU.mult), R=[("memT", kc), "rstd_m", "cst"], W=[("memn", kc)])
            if len(tiles2) > 1:
                sq_part(xsrcs(tiles2[1][0], tiles2[1][1]), sq, "sq", tiles2[1][1])
            p2_qproj(t0_, n_)
            mtoks = [("memn", kc) for kc in range(8)]
            for ch in range(8):
                slot = ch // 4
                w = wslot(slot, 8, 512)
                c0 = (ch % 4) * 128
                pt, ptok = next_ps()
                tk.mm(pt[:, 0:NMEM], [(w[:, kc, c0:c0 + 128], memn[:, kc, :]) for kc in range(8)],
                      R=[("W", slot)] + mtoks, W=[ptok])
                tk.op("act", lambda h, pt=pt, ch=ch: h.copy(out=KT[:, ch, :], in_=pt[:, 0:NMEM]), R=[ptok], W=[("KT", ch)])
            for mc in range(2):
                for nh in range(2):
                    slot = 2 + nh
                    w = wslot(slot, 8, 512)
                    pt, ptok = next_ps()
                    tk.mm(pt[:, :], [(memn[:, kc, mc * 128:(mc + 1) * 128], w[:, kc, :]) for kc in range(8)],
                          R=[("W", slot)] + mtoks, W=[ptok])
                    tk.op("act", lambda h, pt=pt, mc=mc, nh=nh: h.copy(out=Vm[:, mc, nh * 512:(nh + 1) * 512], in_=pt[:, :]),
                          R=[ptok], W=[("Vm", mc, nh)])
            tk.barrier(("pe", "act", "dve"))

            for i, (t0, n) in enumerate(tiles2):
                nxt = tiles2[i + 1] if i + 1 < len(tiles2) else None
                if i > 0:
                    if nxt is not None:
                        sq_part(xsrcs(nxt[0], nxt[1]), sq, "sq", nxt[1])
                    p2_qproj(t0, n)
                pend = []
                for hx in range(4):
                    eis = p2_S(hx, n)
                    pend.append((hx, eis))
                    if len(pend) == 2:
                        ph, pe_ = pend.pop(0)
                        p2_RPV(ph, n, pe_)
                for ph, pe_ in pend:
                    p2_RPV(ph, n, pe_)
                if nxt is not None:
                    rstd_part(8, ones_d[:], "ones_d", nxt[1], sq, "sq", lnb, "lnb", rstd, "rstd")
                    h_part(nxt[0], nxt[1], cb + O_X, rstd, hb, htoks)
                proj_residual((6, 7), onb, [("onb", ii) for ii in range(8)], t0, n)

            tk.barrier(COMPUTE)

            def load_e(e):
                load_w((2 * e) % 8, w_up_d[l][:, :, e * 512:(e + 1) * 512], 8, 512)
                load_w((2 * e + 1) % 8, w_down_d[l][:, 4 * e:4 * e + 4, :], 4, 1024)

            for e in range(4):
                load_e(e)

            cv = Carve()
            A = cv.get([8, T], BF)
            N3 = 256
            sq = cv.get([8, N3], BF)
            lnb = cv.get([1, N3], F32)[:, 0, :]
            rstd = cv.get([1, N3], F32)[:, 0, :]
            hid = cv.get([2, 4, 512], BF)
            ostk = [0]

            def final_tile(i3):
                s0, sn = tiles3[i3]
                for t0 in range(s0, s0 + sn, N3):
                    rms_rstd(xsrcs(t0, N3), ones_d[:], "ones_d", N3, sq, lnb, rstd, "rstd")
                    for kc in range(8):
                        tk.op("dve", lambda h, kc=kc, t0=t0: h.scalar_tensor_tensor(
                            out=xT[:, kc, t0:t0 + N3], in0=xT[:, kc, t0:t0 + N3], scalar=col(C_FIN + kc),
                            in1=rstd[:, 0:N3], op0=ALU.mult, op1=ALU.mult),
                            R=["rstd", "cst"], W=xtoks(kc, t0, N3))
                for kc in range(8):
                    oi = ostk[0] % 3
                    ostk[0] += 1
                    tk.dma("sp", out_d[kc * 128:(kc + 1) * 128, s0 - HALO:s0 - HALO + sn], xT[:, kc, s0:s0 + sn],
                           f"o{oi}", R=xtoks(kc, s0, sn))

            tiles3 = list(tiles2)
            tiles_a = []
            for (s0, sn) in tiles3:
                tcur = s0
                while tcur < s0 + sn:
                    nn = min(N3, s0 + sn - tcur)
                    tiles_a.append((tcur, nn))
                    tcur += nn
            def a_norm(i3):
                if i3 >= len(tiles3):
                    return
                s0, sn = tiles3[i3]
                for (t0, n) in tiles_a:
                    if t0 >= s0 and t0 < s0 + sn:
                        norm_h(t0, n, cb + O_MLP, sq, lnb, rstd, A[:, :, t0:t0 + n],
                               [("A", kc, t0 // 128) for kc in range(8)])


            def a_toks(t0, n):
                ks = set()
                for (s0, sn) in tiles_a:
                    if s0 < t0 + n and s0 + sn > t0:
                        ks.add(s0 // 128)
                return [("A", kc, j) for kc in range(8) for j in sorted(ks)]

            def up(e, i):
                t0, n = tiles3[i]
                hs = (e * len(tiles3) + i) % 2
                wu = wslot((2 * e) % 8, 8, 512)
                for f in range(4):
                    pt, ptok = next_ps()
                    tk.mm(pt[:, 0:n], [(wu[:, kc, f * 128:(f + 1) * 128], A[:, kc, t0:t0 + n]) for kc in range(8)],
                          R=[("W", (2 * e) % 8)] + a_toks(t0, n), W=[ptok])
                    tk.op("act", lambda h, pt=pt, f=f, hs=hs, n=n: h.activation(out=hid[:, hs, f, 0:n], in_=pt[:, 0:n],
                                                                             func=AF.Relu), R=[ptok], W=[("hid", hs, f)])
                    tk.op("dve", lambda h, f=f, hs=hs, n=n: h.tensor_tensor(out=hid[:, hs, f, 0:n], in0=hid[:, hs, f, 0:n],
                                                                         in1=hid[:, hs, f, 0:n], op=ALU.mult),
                          W=[("hid", hs, f)])

            def down(e, i):
                t0, n = tiles3[i]
                hs = (e * len(tiles3) + i) % 2
                slot = (2 * e + 1) % 8
                wd = wslot(slot, 4, 1024)
                for oc in range(8):
                    pt, ptok = next_ps()
                    tk.mm(pt[:, 0:n], [(wd[:, f, oc * 128:(oc + 1) * 128], hid[:, hs, f, 0:n]) for f in range(4)],
                          R=[("W", slot)] + [("hid", hs, f) for f in range(4)], W=[ptok])
                    tk.op("dve", lambda h, oc=oc, pt=pt, t0=t0, n=n: h.tensor_tensor(
                        out=xT[:, oc, t0:t0 + n], in0=pt[:, 0:n], in1=xT[:, oc, t0:t0 + n], op=ALU.add),
                        R=[ptok], W=xtoks(oc, t0, n))

            steps = [(e, i) for e in range(8) for i in range(len(tiles3))]
            last_layer = (l == DEPTH - 1)

            def halves(i3):
                if i3 < 0 or i3 >= len(tiles3):
                    return []
                s0, sn = tiles3[i3]
                return [(t0, n) for (t0, n) in tiles_a if s0 <= t0 < s0 + sn]

            def nsq(hv):
                sq_part(xsrcs(hv[0], hv[1]), sq, "sq", hv[1])

            def nrest_a(hv):
                t0, n = hv
                rstd_part(8, ones_d[:], "ones_d", n, sq, "sq", lnb, "lnb", rstd, "rstd")
                h_part(t0, n, cb + O_MLP, rstd, A[:, :, t0:t0 + n], [("A", kc, t0 // 128) for kc in range(8)])

            def nrest_f(hv):
                t0, n = hv
                rstd_part(8, ones_d[:], "ones_d", n, sq, "sq", lnb, "lnb", rstd, "rstd")
                for kc in range(8):
                    tk.op("dve", lambda h, kc=kc: h.scalar_tensor_tensor(
                        out=xT[:, kc, t0:t0 + n], in0=xT[:, kc, t0:t0 + n], scalar=col(C_FIN + kc),
                        in1=rstd[:, 0:n], op0=ALU.mult, op1=ALU.mult),
                        R=["rstd", "cst"], W=xtoks(kc, t0, n))

            def out_dma(i3):
                s0, sn = tiles3[i3]
                for kc in range(8):
                    oi = ostk[0] % 3
                    ostk[0] += 1
                    tk.dma("sp", out_d[kc * 128:(kc + 1) * 128, s0 - HALO:s0 - HALO + sn], xT[:, kc, s0:s0 + sn],
                           f"o{oi}", R=xtoks(kc, s0, sn))

            a_norm(0)
            a_norm(1)
            up(*steps[0])
            for si, (e, i) in enumerate(steps):
                H, nrest = [], None
                if e == 0:
                    H, nrest = halves(i + 2), nrest_a
                elif last_layer and e == 7:
                    H, nrest = halves(i - 1), nrest_f
                if H:
                    nsq(H[0])
                if si + 1 < len(steps):
                    up(*steps[si + 1])
                if H:
                    nrest(H[0])
                    if len(H) > 1:
                        nsq(H[1])
                down(e, i)
                if len(H) > 1:
                    nrest(H[1])
                if H and nrest is nrest_f:
                    out_dma(i - 1)
                if i == len(tiles3) - 1 and e + 4 < 8:
                    load_e(e + 4)
            if last_layer:
                for hv in halves(len(tiles3) - 1):
                    nsq(hv)
                    nrest_f(hv)
                out_dma(len(tiles3) - 1)

            tk.barrier(COMPUTE)

        sp = tk.eng["sp"]
        for i in range(3):
            d = tk.dsem[f"o{i}"]
            if d[2] > 0:
                sp.h.wait_ge(d[0], d[2])
    return nc


def _prep_shared(norm_mix_g, w_in, conv_w, sinks, gnorm_conv_g, gnorm_attn_g, w_out, norm_x_g, norm_mem_g,
                 wx_q, wx_kv, wx_o, norm_mlp_g, w_up, w_down, final_g):
    f32 = np.float32
    qbase, kbase, vbase = 1536, 2048, 2176
    head_pair = []
    for j in range(4):
        head_pair += list(range(j * 64, j * 64 + 64)) + list(range((4 + j) * 64, (4 + j) * 64 + 64))
    head_pair = np.array(head_pair)

    def swap_halves(cols):
        cols = np.asarray(cols).reshape(-1, 2, 32)
        return cols[:, ::-1, :].reshape(-1)

    qcols = qbase + head_pair
    kcols = kbase + np.arange(128)
    perm = np.concatenate([
        np.arange(512, 1024), np.arange(1024, 1536), np.arange(0, 512),
        qcols, kcols, vbase + np.arange(128)])
    assert perm.shape[0] == WIN_COLS

    def kmajor(w, nk):
        return np.ascontiguousarray(w.reshape(nk, 128, w.shape[1]).transpose(1, 0, 2))

    rows_out = np.concatenate([np.arange(512), 512 + head_pair])
    sh = {}
    sh["w_in"] = np.stack([kmajor(np.asarray(w_in[l], f32)[:, perm], 8) for l in range(DEPTH)])
    sh["w_out"] = np.stack([kmajor(np.asarray(w_out[l], f32)[rows_out, :], 8) for l in range(DEPTH)])
    sh["wx_q"] = np.stack([kmajor(np.asarray(wx_q[l], f32), 8) for l in range(DEPTH)])
    sh["wx_kv"] = np.stack([kmajor(np.asarray(wx_kv[l], f32), 8) for l in range(DEPTH)])
    sh["wx_o"] = np.stack([kmajor(np.asarray(wx_o[l], f32), 8) for l in range(DEPTH)])
    sh["w_up"] = np.stack([kmajor(np.asarray(w_up[l], f32), 8) for l in range(DEPTH)])
    sh["w_down"] = np.stack([kmajor(np.asarray(w_down[l], f32), 32) for l in range(DEPTH)])
    sh["sinks"] = np.ascontiguousarray(np.asarray(sinks, f32).reshape(1, 16))

    cst = np.zeros((128, NCST), f32)
    cst[:, C_EPS] = EPS
    cst[:, C_HPI] = math.pi / 2
    p = np.arange(128)
    jj = (p % 64) % 32
    cst[:, C_INVF] = (10000.0 ** (-(2.0 * jj) / 64.0)).astype(f32)
    cst[:, C_SIGN] = np.where((p % 64) < 32, -1.0, 1.0)

    def cols8(v):
        return np.asarray(v, f32).reshape(-1, 128).T

    for l in range(DEPTH):
        cb = C_L0 + l * C_LSTRIDE
        cst[:, cb + O_MIX:cb + O_MIX + 8] = cols8(norm_mix_g[l])
        cst[:, cb + O_X:cb + O_X + 8] = cols8(norm_x_g[l])
        cst[:, cb + O_MEM:cb + O_MEM + 8] = cols8(norm_mem_g[l])
        cst[:, cb + O_MLP:cb + O_MLP + 8] = cols8(norm_mlp_g[l])
        cst[:, cb + O_GC:cb + O_GC + 4] = cols8(gnorm_conv_g[l])
        cst[:, cb + O_GA:cb + O_GA + 4] = cols8(np.asarray(gnorm_attn_g[l], f32)[head_pair])
        for k in range(3):
            cst[:, cb + O_CW + k * 4:cb + O_CW + k * 4 + 4] = cols8(conv_w[l][k])
    cst[:, C_FIN:C_FIN + 8] = cols8(final_g)
    sh["cst"] = cst
    return sh


_NC_CACHE = {}


def kernel(x, mem, positions, norm_mix_g, w_in, conv_w, sinks, gnorm_conv_g, gnorm_attn_g, w_out, norm_x_g,
           norm_mem_g, wx_q, wx_kv, wx_o, norm_mlp_g, w_up, w_down, final_g):
    f32 = np.float32
    x = np.asarray(x, f32)
    mem = np.asarray(mem, f32)
    positions = np.asarray(positions, np.int32)
    sh = _prep_shared(norm_mix_g, w_in, conv_w, sinks, gnorm_conv_g, gnorm_attn_g, w_out, norm_x_g, norm_mem_g,
                      wx_q, wx_kv, wx_o, norm_mlp_g, w_up, w_down, final_g)
    kk = np.arange(128)[:, None]
    qq = np.arange(128)[None, :]
    m_prev = (kk > qq).astype(f32)
    m_cur = (kk <= qq).astype(f32)
    partner = np.where((np.arange(128) % 64) < 32, np.arange(128) + 32, np.arange(128) - 32)
    pm = np.zeros((128, 128), f32)
    pm[partner, np.arange(128)] = 1.0
    in_maps = []
    for c in range(NCORES):
        b, hh = c // 2, c % 2
        xs = np.zeros((T, D), f32)
        ps = np.zeros((1, T), np.int32)
        if hh == 0:
            xs[HALO:] = x[b, 0:TOK]
            ps[0, HALO:] = positions[b, 0:TOK]
        else:
            xs[:] = x[b, TOK - HALO:SEQ]
            ps[0, :] = positions[b, TOK - HALO:SEQ]
        m_first = m_prev if hh == 1 else np.zeros_like(m_prev)
        masks = np.concatenate([np.stack([np.tile(m, (1, 4)) for m in (m_prev, m_cur, m_first)], axis=1).reshape(128, 3 * 512), np.eye(128, dtype=f32), pm], axis=1)
        cst = sh["cst"].copy()
        cst[:, C_FLAG] = float(hh)
        d = {
            "xT": np.ascontiguousarray(xs.T),
            "pos": ps,
            "memT": np.ascontiguousarray(mem[b].T),
            "cst": cst,
            "sinks": sh["sinks"],
            "masks": np.ascontiguousarray(masks),
        }
        for k in ("w_in", "w_out", "wx_q", "wx_kv", "wx_o", "w_up", "w_down"):
            d[k] = sh[k]
        in_maps.append(d)
    if "nc" not in _NC_CACHE:
        _NC_CACHE["nc"] = build_program()
    nc = _NC_CACHE["nc"]
    res = run_bass_kernel_spmd(nc, in_maps, core_ids=list(range(NCORES)))
    out = np.empty((4, SEQ, D), f32)
    for c in range(NCORES):
        b, hh = c // 2, c % 2
        out[b, hh * TOK:(hh + 1) * TOK, :] = np.asarray(res.results[c]["outT"]).T
    return out
```

```python
import math
from contextlib import ExitStack

import numpy as np
import concourse.bass as bass
import concourse.mybir as mybir
from concourse.bass_utils import run_bass_kernel_spmd

F32 = mybir.dt.float32
BF = mybir.dt.bfloat16
I32 = mybir.dt.int32
AF = mybir.ActivationFunctionType
ALU = mybir.AluOpType

NCORES = 8
DEPTH = 2
D = 1024
SEQ = 4096
TOK = 2048
HALO = 256
T = TOK + HALO
NB = T // 128
NMEM = 256
DFF = 4096
EPS = 1e-6
TWO_PI = 2.0 * math.pi

C_EPS, C_HPI, C_INVF, C_SIGN, C_FLAG = 0, 1, 2, 3, 4
C_L0 = 8
C_LSTRIDE = 52
O_MIX, O_X, O_MEM, O_MLP, O_GC, O_GA, O_CW = 0, 8, 16, 24, 32, 36, 40
C_FIN = C_L0 + DEPTH * C_LSTRIDE
NCST = C_FIN + 8

WIN_COLS = 2304


class _Ev:
    __slots__ = ("sem", "key", "val", "eng", "clock")

    def __init__(self, sem, key, val, eng, clock=None):
        self.sem, self.key, self.val, self.eng = sem, key, val, eng
        self.clock = clock


class _Eng:
    def __init__(self, name, h, sem, key):
        self.name, self.h, self.sem, self.key = name, h, sem, key
        self.n = 0
        self.seen = {}
        self.last_clock = None


class Trk:
    def __init__(self, nc, es):
        self.nc, self.es = nc, es
        self.tok = {}
        self.eng = {}
        self.dsem = {}
        self.nsem = 0

    def _newsem(self, name):
        self.nsem += 1
        return self.es.enter_context(self.nc.semaphore(name)), self.nsem

    def add_engine(self, name, h):
        sem, key = self._newsem("e_" + name)
        self.eng[name] = _Eng(name, h, sem, key)

    def add_dsem(self, name):
        sem, key = self._newsem("d_" + name)
        self.dsem[name] = [sem, key, 0]

    def _st(self, t):
        s = self.tok.get(t)
        if s is None:
            s = [None, {}]
            self.tok[t] = s
        return s

    def _wait(self, e, ev):
        if ev is None:
            return
        if ev.eng == e.name and e.name == "pe":
            return
        if e.seen.get(ev.key, 0) >= ev.val:
            return
        e.h.wait_ge(ev.sem, ev.val)
        e.seen[ev.key] = ev.val
        if ev.clock:
            sn = e.seen
            for k, v in ev.clock.items():
                if sn.get(k, 0) < v:
                    sn[k] = v

    def _deps(self, e, R, W):
        for t in R:
            self._wait(e, self._st(t)[0])
        for t in W:
            st = self._st(t)
            self._wait(e, st[0])
            for ev in st[1].values():
                self._wait(e, ev)

    def _done(self, ev, R, W, rkey):
        for t in R:
            self._st(t)[1][rkey] = ev
        for t in W:
            st = self._st(t)
            st[0] = ev
            st[1] = {}

    def op(self, en, fn, R=(), W=()):
        e = self.eng[en]
        psr = [t for t in R if isinstance(t, tuple) and t[0] == "ps"]
        if psr:
            R = [t for t in R if not (isinstance(t, tuple) and t[0] == "ps")]
            W = list(W) + psr
        self._deps(e, R, W)
        ins = fn(e.h)
        e.n += 1
        ins.then_inc(e.sem, 1)
        e.last_clock = dict(e.seen)
        self._done(_Ev(e.sem, e.key, e.n, en, e.last_clock), R, W, en)

    def mm(self, out_ap, pairs, R=(), W=()):
        e = self.eng["pe"]
        for t in W:
            if isinstance(t, tuple) and t[0] == "ps":
                ev = self._st(t)[0]
                if ev is None or ev.eng not in ("act", "dve") or e.seen.get(ev.key, 0) >= ev.val:
                    continue
                evn = self._st(("ps", (t[1] + 1) % 8))[0]
                if evn is not None and evn.key == ev.key and ev.val < evn.val <= ev.val + 1:
                    self._wait(e, evn)
        self._deps(e, R, W)
        n = len(pairs)
        ins = None
        for i, (l, r) in enumerate(pairs):
            ins = e.h.matmul(out_ap, l, r, start=(i == 0), stop=(i == n - 1))
        e.n += 1
        ins.then_inc(e.sem, 1)
        e.last_clock = dict(e.seen)
        self._done(_Ev(e.sem, e.key, e.n, "pe", e.last_clock), R, W, "pe")

    def dma(self, qn, out, in_, sem, R=(), W=()):
        e = self.eng[qn]
        self._deps(e, R, W)
        d = self.dsem[sem]
        ins = e.h.dma_start(out=out, in_=in_)
        d[2] += 16
        ins.then_inc(d[0], 16)
        self._done(_Ev(d[0], d[1], d[2], "dma"), R, W, ("dma", sem))

    def finalize_sem(self, sem):
        d = self.dsem[sem]
        for st in self.tok.values():
            if st[0] is not None and st[0].key == d[1]:
                st[0] = _Ev(d[0], d[1], d[2], "dma")
            for k, ev in list(st[1].items()):
                if ev.key == d[1]:
                    st[1][k] = _Ev(d[0], d[1], d[2], "dma")

    def barrier(self, names):
        evs = []
        for n in names:
            e = self.eng[n]
            if e.n > 0:
                evs.append(_Ev(e.sem, e.key, e.n, n, e.last_clock))
        for n in names:
            e = self.eng[n]
            for ev in evs:
                if ev.eng != n:
                    self._wait(e, ev)
                elif n != "pe":
                    self._wait(e, ev)


def build_program():
    nc = bass.Bass("TRN2", target_bir_lowering=False)

    def din(name, shape, dt):
        return nc.dram_tensor(name, shape, dt, kind="ExternalInput").ap()

    xT_d = din("xT", [D, T], F32)
    pos_d = din("pos", [1, T], I32)
    memT_d = din("memT", [D, NMEM], F32)
    cst_d = din("cst", [128, NCST], F32)
    sinks_d = din("sinks", [1, 16], F32)
    masks_d = din("masks", [128, 3 * 512 + 256], F32)
    w_in_d = din("w_in", [DEPTH, 128, 8, WIN_COLS], F32)
    w_out_d = din("w_out", [DEPTH, 128, 8, D], F32)
    wx_q_d = din("wx_q", [DEPTH, 128, 8, D], F32)
    wx_kv_d = din("wx_kv", [DEPTH, 128, 8, 2 * D], F32)
    wx_o_d = din("wx_o", [DEPTH, 128, 8, D], F32)
    w_up_d = din("w_up", [DEPTH, 128, 8, DFF], F32)
    w_down_d = din("w_down", [DEPTH, 128, 32, D], F32)
    out_d = nc.dram_tensor("outT", [D, TOK], F32, kind="ExternalOutput").ap()

    with ExitStack() as es:
        def sb(name, shape, dt):
            return es.enter_context(nc.sbuf_tensor(name, shape, dt))

        xT = sb("xT_sb", [128, 8, T], F32)
        cosT = sb("cosT", [128, T], BF)
        sinT = sb("sinT", [128, T], BF)
        mski = sb("mski", [128, 3 * 512 + 256], BF)
        msk = mski[:, 0:1536].rearrange("p (a b) -> p a b", a=3)
        ident = mski[:, 1536:1664]
        permM = mski[:, 1664:1792]
        qb = sb("qb", [128, 3, 256], BF)
        dg = sb("dg", [128, 12, 128], BF)
        cst = sb("cst_sb", [128, NCST], F32)
        sinkexp = sb("sinkexp", [128, 16], F32)
        ones_d = sb("ones_d", [128, 128], BF)
        ones_g = sb("ones_g", [128, 128], BF)
        Wr = sb("Wr", [128, 8, 4096], BF)
        SBYTES = 53248
        S = sb("S", [128, SBYTES // 2], BF)
        psb = [es.enter_context(nc.psum_tensor(f"ps{i}", [128, 512], F32)) for i in range(8)]

        tk = Trk(nc, es)
        tk.add_engine("pe", nc.tensor)
        tk.add_engine("act", nc.scalar)
        tk.add_engine("dve", nc.vector)
        tk.add_engine("pool", nc.gpsimd)
        tk.add_engine("sp", nc.sync)
        for i in range(8):
            tk.add_dsem(f"w{i}")
        tk.add_dsem("ld")
        tk.add_dsem("xa")
        tk.add_dsem("xb")
        tk.add_dsem("xc")
        tk.add_dsem("ldm")
        tk.add_dsem("mem")
        for i in range(3):
            tk.add_dsem(f"o{i}")

        COMPUTE = ("pe", "act", "dve", "sp")

        class Carve:
            def __init__(self):
                self.off = 0

            def get(self, shape, dt):
                esz = 4 if dt in (F32, I32) else 2
                n = int(np.prod(shape))
                nb = n * esz
                a = self.off
                self.off += (nb + 63) // 64 * 64
                assert self.off <= SBYTES, (self.off, SBYTES)
                ap = S[:, a // 2:(a + nb) // 2]
                if dt != BF:
                    ap = ap.bitcast(dt)
                if len(shape) == 2:
                    return ap.rearrange("p (a b) -> p a b", a=shape[0])
                if len(shape) == 3:
                    return ap.rearrange("p (a b c) -> p a b c", a=shape[0], b=shape[1])
                return ap

        psctr = [0]

        ps_held = set()

        def next_ps(hold=False):
            while True:
                i = psctr[0] % 8
                psctr[0] += 1
                if i not in ps_held:
                    break
            if hold:
                ps_held.add(i)
            return psb[i], ("ps", i)

        def ps_release(ptok):
            ps_held.discard(ptok[1])

        def col(c):
            return cst[:, c:c + 1]

        def xtoks(kc, t0, n):
            return [("x", kc, j) for j in range(t0 // 256, (t0 + n + 255) // 256)]

        def wslot(slot, kcn, cols):
            return Wr[:, slot, 0:kcn * cols].rearrange("p (k c) -> p k c", k=kcn)

        def load_w(slot, src_ap, kcn, cols):
            dst = wslot(slot, kcn, cols)
            h = kcn // 2
            tk.dma("pool", dst[:, 0:h, :], src_ap[:, 0:h, :], f"w{slot}", W=[("W", slot)])
            tk.dma("pool", dst[:, h:kcn, :], src_ap[:, h:kcn, :], f"w{slot}", W=[("W", slot)])

        tk.dma("sp", cst[:], cst_d[:], "ld", W=["cst"])
        cv = Carve()
        posi = cv.get([T], I32)
        angf = cv.get([T], F32)
        tmpf = cv.get([T], F32)
        tmpi = cv.get([T], I32)
        rr = cv.get([T], F32)
        sinks_b = cv.get([16], F32)
        maskf = None
        tk.dma("sp", posi, pos_d.partition_broadcast(128), "ld", W=["posi"])
        tk.dma("sp", sinks_b, sinks_d.partition_broadcast(128), "ld", W=["sinks_b"])
        tk.finalize_sem("ld")
        for kc in range(8):
            tk.dma("sp", xT[:, kc, 0:256], xT_d[kc * 128:(kc + 1) * 128, 0:256], "xa", W=[("x", kc, 0)])
        tk.finalize_sem("xa")
        tk.dma("pool", mski[:], masks_d[:], "ldm", W=["msk"])

        tk.op("dve", lambda h: h.memset(ones_d[:], 1.0 / D), W=["ones_d"])
        tk.op("dve", lambda h: h.memset(ones_g[:], 1.0 / 512.0), W=["ones_g"])
        tk.op("dve", lambda h: h.tensor_copy(out=angf, in_=posi), R=["posi"], W=["angf"])
        tk.op("dve", lambda h: h.tensor_scalar(out=angf, in0=angf, scalar1=col(C_INVF), scalar2=None, op0=ALU.mult),
              R=["cst"], W=["angf"])
        for which in (0, 1):
            add = 0.0 if which == 0 else 0.25
            tk.op("dve", lambda h: h.tensor_scalar(out=tmpf, in0=angf, scalar1=1.0 / TWO_PI, scalar2=add,
                                                   op0=ALU.mult, op1=ALU.add), R=["angf"], W=["tmpf"])
            tk.op("dve", lambda h: h.tensor_copy(out=tmpi, in_=tmpf), R=["tmpf"], W=["tmpi"])
            tk.op("dve", lambda h: h.tensor_copy(out=tmpf, in_=tmpi), R=["tmpi"], W=["tmpf"])
            tk.op("dve", lambda h: h.scalar_tensor_tensor(out=rr, in0=tmpf, scalar=-TWO_PI, in1=angf,
                                                          op0=ALU.mult, op1=ALU.add), R=["tmpf", "angf"], W=["rr"])
            if which == 0:
                lo, hi = -3.14159, 3.14159
            else:
                lo, hi = -3.14159 - math.pi / 2, 3.14159 - math.pi / 2
            tk.op("dve", lambda h: h.tensor_scalar(out=rr, in0=rr, scalar1=lo, scalar2=hi, op0=ALU.max, op1=ALU.min),
                  W=["rr"])
            if which == 0:
                tk.op("act", lambda h: h.activation(out=sinT[:], in_=rr, func=AF.Sin, scale=col(C_SIGN)),
                      R=["rr", "cst"], W=["sinT"])
            else:
                tk.op("act", lambda h: h.activation(out=cosT[:], in_=rr, func=AF.Sin, bias=col(C_HPI), scale=1.0),
                      R=["rr", "cst"], W=["cosT"])
        tk.op("act", lambda h: h.activation(out=sinkexp[:], in_=sinks_b, func=AF.Exp), R=["sinks_b"], W=["sinkexp"])
        tk.barrier(COMPUTE)

        def sq_part(srcs, sq, sqn, n):
            for i, (ap, toks) in enumerate(srcs):
                tk.op("act", lambda h, ap=ap, i=i: h.activation(out=sq[:, i, 0:n], in_=ap, func=AF.Square),
                      R=toks, W=[(sqn, i)])

        def rstd_part(ns, ones_ap, ones_tok, n, sq, sqn, lnb, lnbn, rstd, rstd_tok):
            pt, ptok = next_ps()
            tk.mm(pt[:, 0:n], [(ones_ap, sq[:, i, 0:n]) for i in range(ns)],
                  R=[ones_tok] + [(sqn, i) for i in range(ns)], W=[ptok])
            tk.op("act", lambda h: h.activation(out=lnb[:, 0:n], in_=pt[:, 0:n], func=AF.Ln, bias=col(C_EPS), scale=1.0),
                  R=[ptok, "cst"], W=[lnbn])
            tk.op("act", lambda h: h.activation(out=rstd[:, 0:n], in_=lnb[:, 0:n], func=AF.Exp, scale=-0.5),
                  R=[lnbn], W=[rstd_tok])

        def rms_rstd(srcs, ones_ap, ones_tok, n, sq, lnb, rstd, rstd_tok, sqn="sq", lnbn="lnb"):
            sq_part(srcs, sq, sqn, n)
            rstd_part(len(srcs), ones_ap, ones_tok, n, sq, sqn, lnb, lnbn, rstd, rstd_tok)

        def xsrcs(t0, n):
            return [(xT[:, kc, t0:t0 + n], xtoks(kc, t0, n)) for kc in range(8)]

        def h_part(t0, n, gcol0, rstd, hdst, htoks, rstd_tok="rstd"):
            for kc in range(8):
                tk.op("dve", lambda h, kc=kc: h.scalar_tensor_tensor(
                    out=hdst[:, kc, 0:n], in0=xT[:, kc, t0:t0 + n], scalar=col(gcol0 + kc), in1=rstd[:, 0:n],
                    op0=ALU.mult, op1=ALU.mult), R=xtoks(kc, t0, n) + [rstd_tok, "cst"], W=[htoks[kc]])

        def norm_h(t0, n, gcol0, sq, lnb, rstd, hdst, htoks):
            rms_rstd(xsrcs(t0, n), ones_d[:], "ones_d", n, sq, lnb, rstd, "rstd")
            h_part(t0, n, gcol0, rstd, hdst, htoks)

        def proj_residual(wslots, hsrc, htoks, t0, n, nk=8):
            for oc in range(8):
                slot = wslots[oc // 4]
                w = wslot(slot, nk, 512)
                c0 = (oc % 4) * 128
                pt, ptok = next_ps()
                tk.mm(pt[:, 0:n], [(w[:, kc, c0:c0 + 128], hsrc[:, kc, 0:n]) for kc in range(nk)],
                      R=[("W", slot)] + list(htoks), W=[ptok])
                tk.op("dve", lambda h, oc=oc, pt=pt: h.tensor_tensor(
                    out=xT[:, oc, t0:t0 + n], in0=pt[:, 0:n], in1=xT[:, oc, t0:t0 + n], op=ALU.add),
                    R=[ptok], W=xtoks(oc, t0, n))

        for l in range(DEPTH):
            cb = C_L0 + l * C_LSTRIDE
            for s, (c0, nc_) in enumerate([(0, 512), (512, 512), (1024, 512), (1536, 512), (2048, 256)]):
                load_w(s, w_in_d[l][:, :, c0:c0 + nc_], 8, nc_)
            load_w(6, w_out_d[l][:, :, 0:512], 8, 512)
            load_w(7, w_out_d[l][:, :, 512:1024], 8, 512)
            if l == 0:
                tk._wait(tk.eng["sp"], tk._st(("W", 2))[0])
                for kc in range(8):
                    tk.dma("sp", xT[:, kc, 256:768], xT_d[kc * 128:(kc + 1) * 128, 256:768], "xb",
                           W=[("x", kc, 1), ("x", kc, 2)])
                tk.finalize_sem("xb")
                tk._wait(tk.eng["sp"], tk._st(("W", 7))[0])
                for kc in range(8):
                    tk.dma("sp", xT[:, kc, 768:T], xT_d[kc * 128:(kc + 1) * 128, 768:T], "xc",
                           W=[("x", kc, j) for j in range(3, T // 256)])
                tk.finalize_sem("xc")

            cv = Carve()
            N1 = 256
            sq = cv.get([8, N1], BF)
            sqg = cv.get([8, N1], BF)
            lnb = cv.get([1, N1], F32)[:, 0, :]
            lnb2 = cv.get([1, N1], F32)[:, 0, :]
            rstd = cv.get([1, N1], F32)[:, 0, :]
            hb = cv.get([8, N1], BF)
            gcs = cv.get([2, N1], F32)
            pbuf = cv.get([4, N1 + 4], BF)
            ybuf = cv.get([2, N1], F32)
            cvo = cv.get([4, N1], F32)
            rt = cv.get([4, N1], F32)
            qrot = cv.get([4, N1], BF)
            krot = cv.get([3, 128], BF)
            Vr = cv.get([3, 4, 64], BF)
            Eb = cv.get([4, 512], BF)
            den = cv.get([2, 512], F32)
            attn = cv.get([4, N1], F32)
            rstd2 = cv.get([2, N1], F32)
            mixed = cv.get([8, N1], BF)

            tk.op("dve", lambda h: h.memset(pbuf[:, :, 0:2], 0.0), W=[("p", c) for c in range(4)])
            for k in range(3):
                for c in range(4):
                    tk.op("dve", lambda h, k=k, c=c: h.tensor_scalar(
                        out=dg[:, k * 4 + c, :], in0=ident, scalar1=col(cb + O_CW + k * 4 + c), scalar2=None,
                        op0=ALU.mult), R=["msk", "cst"], W=[("dg", k * 4 + c)])
            tk.op("dve", lambda h: h.memset(krot[:], 0.0), W=[("krot", i) for i in range(3)])
            tk.op("dve", lambda h: h.memset(Vr[:, :, 0:4:3, :], 0.0), W=[("Vr", i) for i in range(3)])
            tk.op("dve", lambda h: h.memset(Vr[:, :, 1:3, :], 1.0), W=[("Vr", i) for i in range(3)])

            wgc, wxc, wgb = wslot(0, 8, 512), wslot(1, 8, 512), wslot(2, 8, 512)
            wq = wslot(3, 8, 512)
            wkv = wslot(4, 8, 256)
            htoks = [("h", kc) for kc in range(8)]

            if l == 0:
                tiles1 = [(i * N1, N1, False) for i in range(T // N1)]
            else:
                tiles1 = [(128, 128, True)] + [(i * N1, N1, False) for i in range(1, T // N1)]

            def inproj(w, slot, c0, n, hoff=0, hold=False):
                pt, ptok = next_ps(hold)
                tk.mm(pt[:, 0:n], [(w[:, kc, c0:c0 + 128], hb[:, kc, hoff:hoff + n]) for kc in range(8)],
                      R=[("W", slot)] + htoks, W=[ptok])
                return pt, ptok

            def p1_sq(td):
                t0, n, _ = td
                sq_part(xsrcs(t0, n), sq, "sq", n)

            def p1_normrest(td):
                t0, n, _ = td
                rstd_part(8, ones_d[:], "ones_d", n, sq, "sq", lnb, "lnb", rstd, "rstd")
                h_part(t0, n, cb + O_MIX, rstd, hb, htoks)

            gbps = {}

            def p1_conv_A(td, c, first):
                t0, n, kvonly = td
                g = c % 2
                pa, pat = inproj(wgc, 0, c * 128, n)
                tk.op("act", lambda h: h.copy(out=gcs[:, g, 0:n], in_=pa[:, 0:n]), R=[pat], W=[("gcs", g)])
                pbk, pbt = inproj(wxc, 1, c * 128, n)
                if first:
                    tk.op("dve", lambda h: h.scalar_tensor_tensor(
                        out=pbuf[:, c, 2:2 + n], in0=pbk[:, 0:n], scalar=col(C_FLAG), in1=gcs[:, g, 0:n],
                        op0=ALU.mult, op1=ALU.mult), R=[pbt, ("gcs", g), "cst"], W=[("p", c)])
                else:
                    tk.op("dve", lambda h: h.tensor_tensor(
                        out=pbuf[:, c, 2:2 + n], in0=pbk[:, 0:n], in1=gcs[:, g, 0:n], op=ALU.mult),
                        R=[pbt, ("gcs", g)], W=[("p", c)])
                if not kvonly:
                    gbps[c] = inproj(wgb, 2, c * 128, n, hold=True)

            def p1_conv_B(td, c):
                t0, n, kvonly = td
                g = c % 2
                if not kvonly:
                    py, pyt = next_ps()
                    tk.mm(py[:, 0:n], [(dg[:, k * 4 + c, :], pbuf[:, c, k:k + n]) for k in range(3)],
                          R=[("p", c)] + [("dg", k * 4 + c) for k in range(3)], W=[pyt])
                    tk.op("act", lambda h: h.copy(out=ybuf[:, g, 0:n], in_=py[:, 0:n]), R=[pyt], W=[("y", g)])
                tk.op("dve", lambda h: h.tensor_copy(out=pbuf[:, c, 0:2], in_=pbuf[:, c, n:n + 2]), W=[("p", c)])
                if not kvonly:
                    pg, pgt = gbps.pop(c)
                    tk.op("dve", lambda h: h.tensor_tensor(
                        out=cvo[:, c, 0:n], in0=pg[:, 0:n], in1=ybuf[:, g, 0:n], op=ALU.mult),
                        R=[pgt, ("y", g)], W=[("cvo", c)])
                    ps_release(pgt)

            def p1_conv_step(td, c, first=False):
                if c < 4:
                    p1_conv_A(td, c, first)
                if c >= 1:
                    p1_conv_B(td, c - 1)

            def rotary(pq, pqt, pqs, pqst, dst, dtok, r, poff, ncols, tt0):
                tk.op("dve", lambda h: h.tensor_tensor(out=rt[:, 2 * r, 0:ncols], in0=pq[:, poff:poff + ncols],
                                                       in1=cosT[:, tt0:tt0 + ncols], op=ALU.mult),
                      R=[pqt, "cosT"], W=[("rt", 2 * r)])
                tk.op("dve", lambda h: h.tensor_tensor(out=rt[:, 2 * r + 1, 0:ncols], in0=pqs[:, poff:poff + ncols],
                                                       in1=sinT[:, tt0:tt0 + ncols], op=ALU.mult),
                      R=[pqst, "sinT"], W=[("rt", 2 * r + 1)])
                tk.op("dve", lambda h: h.tensor_tensor(out=dst, in0=rt[:, 2 * r, 0:ncols],
                                                        in1=rt[:, 2 * r + 1, 0:ncols], op=ALU.add),
                      R=[("rt", 2 * r), ("rt", 2 * r + 1)], W=[dtok])

            def _rot_flush(item, n):
                pq, pqt, qi, outs = item
                pqs, pqst = next_ps()
                tk.mm(pqs[:, 0:n], [(permM, qb[:, qi, 0:n])], R=[("qb", qi), "msk"], W=[pqst])
                for (dst, dtok, r, poff, ncols, tt0) in outs:
                    rotary(pq, pqt, pqs, pqst, dst, dtok, r, poff, ncols, tt0)
                ps_release(pqt)

            def p1_q(td):
                t0, n, kvonly = td
                if kvonly:
                    return
                pend = []
                for j in range(4):
                    pq, pqt = inproj(wq, 3, j * 128, n, hold=True)
                    qi = j % 3
                    tk.op("act", lambda h: h.copy(out=qb[:, qi, 0:n], in_=pq[:, 0:n]), R=[pqt], W=[("qb", qi)])
                    pend.append((pq, pqt, qi, [(qrot[:, j, 0:n], ("qrot", j), j % 2, 0, n, t0)]))
                    if len(pend) == 3:
                        _rot_flush(pend.pop(0), n)
                while pend:
                    _rot_flush(pend.pop(0), n)

            def p1_kv(td):
                t0, n, kvonly = td
                pk, pkt = inproj(wkv, 4, 0, n, hold=True)
                tk.op("act", lambda h: h.copy(out=qb[:, 0, 0:n], in_=pk[:, 0:n]), R=[pkt], W=[("qb", 0)])
                kouts = []
                for b in range(n // 128):
                    B = (t0 + b * 128) // 128
                    ri = B % 3
                    kouts.append((krot[:, ri, :], ("krot", ri), b % 2, b * 128, 128, t0 + b * 128))
                for b in range(n // 128):
                    B = (t0 + b * 128) // 128
                    ri = B % 3
                    pv, pvt = next_ps()
                    tk.mm(pv[:, 0:128], [(hb[:, kc, b * 128:(b + 1) * 128], wkv[:, kc, 128:256]) for kc in range(8)],
                          R=[("W", 4)] + htoks, W=[pvt])
                    tk.op("act", lambda h, pv=pv, ri=ri: h.copy(
                        out=Vr[:, ri, 0:4:3, :], in_=pv[:, 0:128].rearrange("p (a b) -> p a b", a=2)),
                        R=[pvt], W=[("Vr", ri)])
                _rot_flush((pk, pkt, 0, kouts), n)

            def p1_S(td, b, g):
                t0, n, _ = td
                B = t0 // 128 + b
                rc, rp = B % 3, (B - 1) % 3
                orow = slice(64 * g, 64 * g + 64)
                qr = qrot[orow, :, b * 128:(b + 1) * 128]
                qtoks = [("qrot", j) for j in range(4)]
                ets = []
                for which, rix in ((0, rp), (1, rc)):
                    pt, ptok = next_ps()
                    tk.mm(pt[:, :], [(krot[orow, rix, :], qr)], R=[("krot", rix)] + qtoks, W=[ptok])
                    ei = 2 * g + which
                    tk.op("act", lambda h, pt=pt, ei=ei: h.activation(out=Eb[:, ei, :], in_=pt[:, :],
                                                                      func=AF.Exp, scale=0.125),
                          R=[ptok], W=[("E", ei)])
                    mi = 1 if which == 1 else (2 if B == 2 else 0)
                    tk.op("dve", lambda h, ei=ei, mi=mi: h.tensor_tensor(out=Eb[:, ei, :], in0=Eb[:, ei, :],
                                                                          in1=msk[:, mi, :], op=ALU.mult),
                          R=["msk"], W=[("E", ei)])
                    ets.append((ei, rix))
                return ets

            def p1_PV(td, b, g, ets):
                orow = slice(64 * g, 64 * g + 64)
                srow = slice(64 * (1 - g), 64 * (1 - g) + 64)
                po, pot = next_ps()
                tk.mm(po[:, :], [(Vr[:, rix, 2 * g:2 * g + 2, :].rearrange("p a b -> p (a b)"), Eb[:, ei, :])
                                 for (ei, rix) in ets],
                      R=[("E", ei) for ei, _ in ets] + [("Vr", rix) for _, rix in ets], W=[pot])
                sk = sinkexp[srow, l * 8 + 4 * g:l * 8 + 4 * g + 4].unsqueeze(2).to_broadcast([64, 4, 128])
                tk.op("dve", lambda h: h.tensor_tensor(
                    out=den[orow, b, :].rearrange("p (a b) -> p a b", a=4),
                    in0=po[srow, :].rearrange("p (a b) -> p a b", a=4), in1=sk, op=ALU.add),
                    R=[pot, "sinkexp"], W=[("den", b, g)])
                tk.op("act", lambda h: h.copy(out=attn[orow, :, b * 128:(b + 1) * 128],
                                              in_=po[orow, :].rearrange("p (a b) -> p a b", a=4)),
                      R=[pot], W=[("attn", g, b)])

            def p1_attn_norm(td):
                for b in range(2):
                    dt_ = [("den", b, 0), ("den", b, 1)]
                    tk.op("act", lambda h: h.activation(out=den[:, b, :], in_=den[:, b, :], func=AF.Ln), W=dt_)
                    tk.op("act", lambda h: h.activation(out=den[:, b, :], in_=den[:, b, :], func=AF.Exp, scale=-1.0), W=dt_)
                    tk.op("dve", lambda h: h.tensor_tensor(
                        out=attn[:, :, b * 128:(b + 1) * 128], in0=attn[:, :, b * 128:(b + 1) * 128],
                        in1=den[:, b, :].rearrange("p (a b) -> p a b", a=4), op=ALU.mult),
                        R=dt_, W=[("attn", 0, b), ("attn", 1, b)])

            def p1_attn(td, nx):
                u = [(b, g) for b in range(2) for g in range(2)]

                def cstep(c):
                    if nx is not None:
                        p1_conv_step(nx, c)

                e0 = p1_S(td, *u[0])
                e1 = p1_S(td, *u[1])
                cstep(0)
                p1_PV(td, u[0][0], u[0][1], e0)
                e2 = p1_S(td, *u[2])
                p1_gconv_rest(td)
                cstep(1)
                p1_PV(td, u[1][0], u[1][1], e1)
                e3 = p1_S(td, *u[3])
                cstep(2)
                p1_PV(td, u[2][0], u[2][1], e2)
                p1_PV(td, u[3][0], u[3][1], e3)
                cstep(3)
                p1_attn_norm(td)
                cstep(4)

            atoks_all = [("attn", g, b) for g in range(2) for b in range(2)]

            def p1_gconv_sq(td):
                n = td[1]
                sq_part([(cvo[:, c, 0:n], [("cvo", c)]) for c in range(4)], sqg, "sqg", n)

            def p1_gconv_rest(td):
                n = td[1]
                rstd_part(4, ones_g[:], "ones_g", n, sqg, "sqg", lnb2, "lnb2", rstd2[:, 0, :], ("rstd2", 0))
                for c in range(4):
                    tk.op("dve", lambda h, c=c: h.scalar_tensor_tensor(
                        out=mixed[:, c, 0:n], in0=cvo[:, c, 0:n], scalar=col(cb + O_GC + c), in1=rstd2[:, 0, 0:n],
                        op0=ALU.mult, op1=ALU.mult), R=[("cvo", c), ("rstd2", 0), "cst"], W=[("mixed", c)])

            def p1_gattn_sq(td):
                n = td[1]
                for j in range(4):
                    tk.op("act", lambda h, j=j: h.activation(out=sqg[:, 4 + j, 0:n], in_=attn[:, j, 0:n], func=AF.Square),
                          R=atoks_all, W=[("sqg", 4 + j)])

            def p1_gattn(td):
                n = td[1]
                pt, ptok = next_ps()
                tk.mm(pt[:, 0:n], [(ones_g[:], sqg[:, 4 + j, 0:n]) for j in range(4)],
                      R=["ones_g"] + [("sqg", 4 + j) for j in range(4)], W=[ptok])
                tk.op("act", lambda h: h.activation(out=lnb2[:, 0:n], in_=pt[:, 0:n], func=AF.Ln, bias=col(C_EPS), scale=1.0),
                      R=[ptok, "cst"], W=["lnb2"])
                tk.op("act", lambda h: h.activation(out=rstd2[:, 1, 0:n], in_=lnb2[:, 0:n], func=AF.Exp, scale=-0.5),
                      R=["lnb2"], W=[("rstd2", 1)])
                for j in range(4):
                    tk.op("dve", lambda h, j=j: h.scalar_tensor_tensor(
                        out=mixed[:, 4 + j, 0:n], in0=attn[:, j, 0:n], scalar=col(cb + O_GA + j), in1=rstd2[:, 1, 0:n],
                        op0=ALU.mult, op1=ALU.mult), R=atoks_all + [("rstd2", 1), "cst"], W=[("mixed", 4 + j)])

            def tl(i):
                return tiles1[i] if i < len(tiles1) else None

            p1_sq(tl(0))
            p1_normrest(tl(0))
            p1_sq(tl(1))
            for c in range(5):
                p1_conv_step(tl(0), c, True)
            p1_q(tl(0))
            p1_kv(tl(0))
            p1_normrest(tl(1))
            p1_sq(tl(2))
            if not tl(0)[2]:
                p1_gconv_sq(tl(0))
            for i, td in enumerate(tiles1):
                nx, nx2, nx3 = tl(i + 1), tl(i + 2), tl(i + 3)
                kvonly = td[2]
                if not kvonly:
                    p1_attn(td, nx)
                elif nx is not None:
                    for c in range(5):
                        p1_conv_step(nx, c)
                if nx is not None:
                    p1_q(nx)
                if not kvonly:
                    p1_gattn_sq(td)
                    p1_gattn(td)
                if nx is not None:
                    p1_kv(nx)
                if nx2 is not None:
                    p1_normrest(nx2)
                if nx3 is not None:
                    p1_sq(nx3)
                if nx is not None:
                    p1_gconv_sq(nx)
                if not kvonly:
                    proj_residual((6, 7), mixed, [("mixed", ii) for ii in range(8)], td[0], td[1])

            tk.barrier(COMPUTE)

            load_w(5, wx_q_d[l][:, :, 512:1024], 8, 512)
            load_w(4, wx_q_d[l][:, :, 0:512], 8, 512)
            for s in range(4):
                load_w(s, wx_kv_d[l][:, :, s * 512:(s + 1) * 512], 8, 512)
            load_w(6, wx_o_d[l][:, :, 0:512], 8, 512)
            load_w(7, wx_o_d[l][:, :, 512:1024], 8, 512)

            cv = Carve()
            N2 = 512
            KT = cv.get([8, NMEM], BF)
            Vm = cv.get([2, D], BF)
            sq = cv.get([8, N2], BF)
            lnb = cv.get([1, N2], F32)[:, 0, :]
            rstd = cv.get([1, N2], F32)[:, 0, :]
            hb = cv.get([8, N2], BF)
            qT = cv.get([8, N2], BF)
            cv_mark = cv.off
            memT = cv.get([8, NMEM], F32)
            memn = cv.get([8, NMEM], BF)
            lnb_m = cv.get([1, NMEM], F32)[:, 0, :]
            rstd_m = cv.get([1, NMEM], F32)[:, 0, :]
            cv.off = cv_mark
            E2 = cv.get([4, N2], BF)
            rec = cv.get([2, N2], F32)
            onb = cv.get([8, N2], BF)
            htoks = [("h", kc) for kc in range(8)]
            tstart = 128 if l == 0 else 256
            if l == 0:
                tiles2 = [(128, 448), (576, 448), (1024, 448), (1472, 448), (1920, 384)]
            else:
                tiles2 = [(256 + 512 * i, 512) for i in range(4)]

            def p2_qproj(t0, n):
                for ch in range(8):
                    slot = 4 + ch // 4
                    w = wslot(slot, 8, 512)
                    c0 = (ch % 4) * 128
                    pt, ptok = next_ps()
                    tk.mm(pt[:, 0:n], [(w[:, kc, c0:c0 + 128], hb[:, kc, 0:n]) for kc in range(8)],
                          R=[("W", slot)] + htoks, W=[ptok])
                    tk.op("act", lambda h, pt=pt, ch=ch: h.copy(out=qT[:, ch, 0:n], in_=pt[:, 0:n]), R=[ptok], W=[("qT", ch)])

            def p2_S(hx, n):
                eis = []
                for mc in range(2):
                    pt, ptok = next_ps()
                    tk.mm(pt[:, 0:n], [(KT[:, hx * 2 + dc, mc * 128:(mc + 1) * 128], qT[:, hx * 2 + dc, 0:n])
                                       for dc in range(2)],
                          R=[("KT", hx * 2), ("KT", hx * 2 + 1), ("qT", hx * 2), ("qT", hx * 2 + 1)], W=[ptok])
                    ei = (hx % 2) * 2 + mc
                    tk.op("act", lambda h, pt=pt, ei=ei: h.activation(out=E2[:, ei, 0:n], in_=pt[:, 0:n], func=AF.Exp,
                                                                      scale=1.0 / 16.0), R=[ptok], W=[("E2", ei)])
                    eis.append(ei)
                return eis

            def p2_RPV(hx, n, eis):
                etoks = [("E2", ei) for ei in eis]
                pr, prt = next_ps()
                tk.mm(pr[:, 0:n], [(ones_d[:], E2[:, ei, 0:n]) for ei in eis], R=etoks + ["ones_d"], W=[prt])
                ri = hx % 2
                tk.op("act", lambda h: h.activation(out=rec[:, ri, 0:n], in_=pr[:, 0:n], func=AF.Ln, scale=float(D)),
                      R=[prt], W=[("rec", ri)])
                tk.op("act", lambda h: h.activation(out=rec[:, ri, 0:n], in_=rec[:, ri, 0:n], func=AF.Exp, scale=-1.0),
                      W=[("rec", ri)])
                for dc in range(2):
                    po, pot = next_ps()
                    tk.mm(po[:, 0:n], [(Vm[:, mc, hx * 256 + dc * 128: hx * 256 + (dc + 1) * 128], E2[:, eis[mc], 0:n])
                                       for mc in range(2)],
                          R=etoks + [("Vm", mc, nh) for mc in range(2) for nh in range(2)], W=[pot])
                    tk.op("dve", lambda h, po=po, dc=dc: h.tensor_tensor(
                        out=onb[:, hx * 2 + dc, 0:n], in0=po[:, 0:n], in1=rec[:, ri, 0:n], op=ALU.mult),
                        R=[pot, ("rec", ri)], W=[("onb", hx * 2 + dc)])

            for kc in range(8):
                tk.dma("sp", memT[:, kc, :], memT_d[kc * 128:(kc + 1) * 128, :], "mem", W=[("memT", kc)])
            tk.finalize_sem("mem")
            t0_, n_ = tiles2[0]
            norm_h(t0_, n_, cb + O_X, sq, lnb, rstd, hb, htoks)
            sq_part([(memT[:, kc, :], [("memT", kc)]) for kc in range(8)], sq, "sq", NMEM)
            rstd_part(8, ones_d[:], "ones_d", NMEM, sq, "sq", lnb_m, "lnb_m", rstd_m, "rstd_m")
            for kc in range(8):
                tk.op("dve", lambda h, kc=kc: h.scalar_tensor_tensor(
                    out=memn[:, kc, :], in0=memT[:, kc, :], scalar=col(cb + O_MEM + kc), in1=rstd_m[:, 0:NMEM],
                    op0=ALU.mult, op1=ALU.mult), R=[("memT", kc), "rstd_m", "cst"], W=[("memn", kc)])
            if len(tiles2) > 1:
                sq_part(xsrcs(tiles2[1][0], tiles2[1][1]), sq, "sq", tiles2[1][1])
            p2_qproj(t0_, n_)
            mtoks = [("memn", kc) for kc in range(8)]
            for ch in range(8):
                slot = ch // 4
                w = wslot(slot, 8, 512)
                c0 = (ch % 4) * 128
                pt, ptok = next_ps()
                tk.mm(pt[:, 0:NMEM], [(w[:, kc, c0:c0 + 128], memn[:, kc, :]) for kc in range(8)],
                      R=[("W", slot)] + mtoks, W=[ptok])
                tk.op("act", lambda h, pt=pt, ch=ch: h.copy(out=KT[:, ch, :], in_=pt[:, 0:NMEM]), R=[ptok], W=[("KT", ch)])
            for mc in range(2):
                for nh in range(2):
                    slot = 2 + nh
                    w = wslot(slot, 8, 512)
                    pt, ptok = next_ps()
                    tk.mm(pt[:, :], [(memn[:, kc, mc * 128:(mc + 1) * 128], w[:, kc, :]) for kc in range(8)],
                          R=[("W", slot)] + mtoks, W=[ptok])
                    tk.op("act", lambda h, pt=pt, mc=mc, nh=nh: h.copy(out=Vm[:, mc, nh * 512:(nh + 1) * 512], in_=pt[:, :]),
                          R=[ptok], W=[("Vm", mc, nh)])
            tk.barrier(("pe", "act", "dve"))

            for i, (t0, n) in enumerate(tiles2):
                nxt = tiles2[i + 1] if i + 1 < len(tiles2) else None
                if i > 0:
                    if nxt is not None:
                        sq_part(xsrcs(nxt[0], nxt[1]), sq, "sq", nxt[1])
                    p2_qproj(t0, n)
                pend = []
                for hx in range(4):
                    eis = p2_S(hx, n)
                    pend.append((hx, eis))
                    if len(pend) == 2:
                        ph, pe_ = pend.pop(0)
                        p2_RPV(ph, n, pe_)
                for ph, pe_ in pend:
                    p2_RPV(ph, n, pe_)
                if nxt is not None:
                    rstd_part(8, ones_d[:], "ones_d", nxt[1], sq, "sq", lnb, "lnb", rstd, "rstd")
                    h_part(nxt[0], nxt[1], cb + O_X, rstd, hb, htoks)
                proj_residual((6, 7), onb, [("onb", ii) for ii in range(8)], t0, n)

            tk.barrier(COMPUTE)

            def load_e(e):
                load_w((2 * e) % 8, w_up_d[l][:, :, e * 512:(e + 1) * 512], 8, 512)
                load_w((2 * e + 1) % 8, w_down_d[l][:, 4 * e:4 * e + 4, :], 4, 1024)

            for e in range(4):
                load_e(e)

            cv = Carve()
            A = cv.get([8, T], BF)
            N3 = 256
            sq = cv.get([8, N3], BF)
            lnb = cv.get([1, N3], F32)[:, 0, :]
            rstd = cv.get([1, N3], F32)[:, 0, :]
            hid = cv.get([2, 4, 512], BF)
            ostk = [0]

            def final_tile(i3):
                s0, sn = tiles3[i3]
                for t0 in range(s0, s0 + sn, N3):
                    rms_rstd(xsrcs(t0, N3), ones_d[:], "ones_d", N3, sq, lnb, rstd, "rstd")
                    for kc in range(8):
                        tk.op("dve", lambda h, kc=kc, t0=t0: h.scalar_tensor_tensor(
                            out=xT[:, kc, t0:t0 + N3], in0=xT[:, kc, t0:t0 + N3], scalar=col(C_FIN + kc),
                            in1=rstd[:, 0:N3], op0=ALU.mult, op1=ALU.mult),
                            R=["rstd", "cst"], W=xtoks(kc, t0, N3))
                for kc in range(8):
                    oi = ostk[0] % 3
                    ostk[0] += 1
                    tk.dma("sp", out_d[kc * 128:(kc + 1) * 128, s0 - HALO:s0 - HALO + sn], xT[:, kc, s0:s0 + sn],
                           f"o{oi}", R=xtoks(kc, s0, sn))

            tiles3 = list(tiles2)
            tiles_a = []
            for (s0, sn) in tiles3:
                tcur = s0
                while tcur < s0 + sn:
                    nn = min(N3, s0 + sn - tcur)
                    tiles_a.append((tcur, nn))
                    tcur += nn
            def a_norm(i3):
                if i3 >= len(tiles3):
                    return
                s0, sn = tiles3[i3]
                for (t0, n) in tiles_a:
                    if t0 >= s0 and t0 < s0 + sn:
                        norm_h(t0, n, cb + O_MLP, sq, lnb, rstd, A[:, :, t0:t0 + n],
                               [("A", kc, t0 // 128) for kc in range(8)])


            def a_toks(t0, n):
                ks = set()
                for (s0, sn) in tiles_a:
                    if s0 < t0 + n and s0 + sn > t0:
                        ks.add(s0 // 128)
                return [("A", kc, j) for kc in range(8) for j in sorted(ks)]

            def up(e, i):
                t0, n = tiles3[i]
                hs = (e * len(tiles3) + i) % 2
                wu = wslot((2 * e) % 8, 8, 512)
                for f in range(4):
                    pt, ptok = next_ps()
                    tk.mm(pt[:, 0:n], [(wu[:, kc, f * 128:(f + 1) * 128], A[:, kc, t0:t0 + n]) for kc in range(8)],
                          R=[("W", (2 * e) % 8)] + a_toks(t0, n), W=[ptok])
                    tk.op("act", lambda h, pt=pt, f=f, hs=hs, n=n: h.activation(out=hid[:, hs, f, 0:n], in_=pt[:, 0:n],
                                                                             func=AF.Relu), R=[ptok], W=[("hid", hs, f)])
                    tk.op("dve", lambda h, f=f, hs=hs, n=n: h.tensor_tensor(out=hid[:, hs, f, 0:n], in0=hid[:, hs, f, 0:n],
                                                                         in1=hid[:, hs, f, 0:n], op=ALU.mult),
                          W=[("hid", hs, f)])

            def down(e, i):
                t0, n = tiles3[i]
                hs = (e * len(tiles3) + i) % 2
                slot = (2 * e + 1) % 8
                wd = wslot(slot, 4, 1024)
                for oc in range(8):
                    pt, ptok = next_ps()
                    tk.mm(pt[:, 0:n], [(wd[:, f, oc * 128:(oc + 1) * 128], hid[:, hs, f, 0:n]) for f in range(4)],
                          R=[("W", slot)] + [("hid", hs, f) for f in range(4)], W=[ptok])
                    tk.op("dve", lambda h, oc=oc, pt=pt, t0=t0, n=n: h.tensor_tensor(
                        out=xT[:, oc, t0:t0 + n], in0=pt[:, 0:n], in1=xT[:, oc, t0:t0 + n], op=ALU.add),
                        R=[ptok], W=xtoks(oc, t0, n))

            steps = [(e, i) for e in range(8) for i in range(len(tiles3))]
            last_layer = (l == DEPTH - 1)

            def halves(i3):
                if i3 < 0 or i3 >= len(tiles3):
                    return []
                s0, sn = tiles3[i3]
                return [(t0, n) for (t0, n) in tiles_a if s0 <= t0 < s0 + sn]

            def nsq(hv):
                sq_part(xsrcs(hv[0], hv[1]), sq, "sq", hv[1])

            def nrest_a(hv):
                t0, n = hv
                rstd_part(8, ones_d[:], "ones_d", n, sq, "sq", lnb, "lnb", rstd, "rstd")
                h_part(t0, n, cb + O_MLP, rstd, A[:, :, t0:t0 + n], [("A", kc, t0 // 128) for kc in range(8)])

            def nrest_f(hv):
                t0, n = hv
                rstd_part(8, ones_d[:], "ones_d", n, sq, "sq", lnb, "lnb", rstd, "rstd")
                for kc in range(8):
                    tk.op("dve", lambda h, kc=kc: h.scalar_tensor_tensor(
                        out=xT[:, kc, t0:t0 + n], in0=xT[:, kc, t0:t0 + n], scalar=col(C_FIN + kc),
                        in1=rstd[:, 0:n], op0=ALU.mult, op1=ALU.mult),
                        R=["rstd", "cst"], W=xtoks(kc, t0, n))

            def out_dma(i3):
                s0, sn = tiles3[i3]
                for kc in range(8):
                    oi = ostk[0] % 3
                    ostk[0] += 1
                    tk.dma("sp", out_d[kc * 128:(kc + 1) * 128, s0 - HALO:s0 - HALO + sn], xT[:, kc, s0:s0 + sn],
                           f"o{oi}", R=xtoks(kc, s0, sn))

            a_norm(0)
            a_norm(1)
            up(*steps[0])
            for si, (e, i) in enumerate(steps):
                H, nrest = [], None
                if e == 0:
                    H, nrest = halves(i + 2), nrest_a
                elif last_layer and e == 7:
                    H, nrest = halves(i - 1), nrest_f
                if H:
                    nsq(H[0])
                if si + 1 < len(steps):
                    up(*steps[si + 1])
                if H:
                    nrest(H[0])
                    if len(H) > 1:
                        nsq(H[1])
                down(e, i)
                if len(H) > 1:
                    nrest(H[1])
                if H and nrest is nrest_f:
                    out_dma(i - 1)
                if i == len(tiles3) - 1 and e + 4 < 8:
                    load_e(e + 4)
            if last_layer:
                for hv in halves(len(tiles3) - 1):
                    nsq(hv)
                    nrest_f(hv)
                out_dma(len(tiles3) - 1)

            tk.barrier(COMPUTE)

        sp = tk.eng["sp"]
        for i in range(3):
            d = tk.dsem[f"o{i}"]
            if d[2] > 0:
                sp.h.wait_ge(d[0], d[2])
    return nc


def _prep_shared(norm_mix_g, w_in, conv_w, sinks, gnorm_conv_g, gnorm_attn_g, w_out, norm_x_g, norm_mem_g,
                 wx_q, wx_kv, wx_o, norm_mlp_g, w_up, w_down, final_g):
    f32 = np.float32
    qbase, kbase, vbase = 1536, 2048, 2176
    head_pair = []
    for j in range(4):
        head_pair += list(range(j * 64, j * 64 + 64)) + list(range((4 + j) * 64, (4 + j) * 64 + 64))
    head_pair = np.array(head_pair)

    def swap_halves(cols):
        cols = np.asarray(cols).reshape(-1, 2, 32)
        return cols[:, ::-1, :].reshape(-1)

    qcols = qbase + head_pair
    kcols = kbase + np.arange(128)
    perm = np.concatenate([
        np.arange(512, 1024), np.arange(1024, 1536), np.arange(0, 512),
        qcols, kcols, vbase + np.arange(128)])
    assert perm.shape[0] == WIN_COLS

    def kmajor(w, nk):
        return np.ascontiguousarray(w.reshape(nk, 128, w.shape[1]).transpose(1, 0, 2))

    rows_out = np.concatenate([np.arange(512), 512 + head_pair])
    sh = {}
    sh["w_in"] = np.stack([kmajor(np.asarray(w_in[l], f32)[:, perm], 8) for l in range(DEPTH)])
    sh["w_out"] = np.stack([kmajor(np.asarray(w_out[l], f32)[rows_out, :], 8) for l in range(DEPTH)])
    sh["wx_q"] = np.stack([kmajor(np.asarray(wx_q[l], f32), 8) for l in range(DEPTH)])
    sh["wx_kv"] = np.stack([kmajor(np.asarray(wx_kv[l], f32), 8) for l in range(DEPTH)])
    sh["wx_o"] = np.stack([kmajor(np.asarray(wx_o[l], f32), 8) for l in range(DEPTH)])
    sh["w_up"] = np.stack([kmajor(np.asarray(w_up[l], f32), 8) for l in range(DEPTH)])
    sh["w_down"] = np.stack([kmajor(np.asarray(w_down[l], f32), 32) for l in range(DEPTH)])
    sh["sinks"] = np.ascontiguousarray(np.asarray(sinks, f32).reshape(1, 16))

    cst = np.zeros((128, NCST), f32)
    cst[:, C_EPS] = EPS
    cst[:, C_HPI] = math.pi / 2
    p = np.arange(128)
    jj = (p % 64) % 32
    cst[:, C_INVF] = (10000.0 ** (-(2.0 * jj) / 64.0)).astype(f32)
    cst[:, C_SIGN] = np.where((p % 64) < 32, -1.0, 1.0)

    def cols8(v):
        return np.asarray(v, f32).reshape(-1, 128).T

    for l in range(DEPTH):
        cb = C_L0 + l * C_LSTRIDE
        cst[:, cb + O_MIX:cb + O_MIX + 8] = cols8(norm_mix_g[l])
        cst[:, cb + O_X:cb + O_X + 8] = cols8(norm_x_g[l])
        cst[:, cb + O_MEM:cb + O_MEM + 8] = cols8(norm_mem_g[l])
        cst[:, cb + O_MLP:cb + O_MLP + 8] = cols8(norm_mlp_g[l])
        cst[:, cb + O_GC:cb + O_GC + 4] = cols8(gnorm_conv_g[l])
        cst[:, cb + O_GA:cb + O_GA + 4] = cols8(np.asarray(gnorm_attn_g[l], f32)[head_pair])
        for k in range(3):
            cst[:, cb + O_CW + k * 4:cb + O_CW + k * 4 + 4] = cols8(conv_w[l][k])
    cst[:, C_FIN:C_FIN + 8] = cols8(final_g)
    sh["cst"] = cst
    return sh


_NC_CACHE = {}


def kernel(x, mem, positions, norm_mix_g, w_in, conv_w, sinks, gnorm_conv_g, gnorm_attn_g, w_out, norm_x_g,
           norm_mem_g, wx_q, wx_kv, wx_o, norm_mlp_g, w_up, w_down, final_g):
    f32 = np.float32
    x = np.asarray(x, f32)
    mem = np.asarray(mem, f32)
    positions = np.asarray(positions, np.int32)
    sh = _prep_shared(norm_mix_g, w_in, conv_w, sinks, gnorm_conv_g, gnorm_attn_g, w_out, norm_x_g, norm_mem_g,
                      wx_q, wx_kv, wx_o, norm_mlp_g, w_up, w_down, final_g)
    kk = np.arange(128)[:, None]
    qq = np.arange(128)[None, :]
    m_prev = (kk > qq).astype(f32)
    m_cur = (kk <= qq).astype(f32)
    partner = np.where((np.arange(128) % 64) < 32, np.arange(128) + 32, np.arange(128) - 32)
    pm = np.zeros((128, 128), f32)
    pm[partner, np.arange(128)] = 1.0
    in_maps = []
    for c in range(NCORES):
        b, hh = c // 2, c % 2
        xs = np.zeros((T, D), f32)
        ps = np.zeros((1, T), np.int32)
        if hh == 0:
            xs[HALO:] = x[b, 0:TOK]
            ps[0, HALO:] = positions[b, 0:TOK]
        else:
            xs[:] = x[b, TOK - HALO:SEQ]
            ps[0, :] = positions[b, TOK - HALO:SEQ]
        m_first = m_prev if hh == 1 else np.zeros_like(m_prev)
        masks = np.concatenate([np.stack([np.tile(m, (1, 4)) for m in (m_prev, m_cur, m_first)], axis=1).reshape(128, 3 * 512), np.eye(128, dtype=f32), pm], axis=1)
        cst = sh["cst"].copy()
        cst[:, C_FLAG] = float(hh)
        d = {
            "xT": np.ascontiguousarray(xs.T),
            "pos": ps,
            "memT": np.ascontiguousarray(mem[b].T),
            "cst": cst,
            "sinks": sh["sinks"],
            "masks": np.ascontiguousarray(masks),
        }
        for k in ("w_in", "w_out", "wx_q", "wx_kv", "wx_o", "w_up", "w_down"):
            d[k] = sh[k]
        in_maps.append(d)
    if "nc" not in _NC_CACHE:
        _NC_CACHE["nc"] = build_program()
    nc = _NC_CACHE["nc"]
    res = run_bass_kernel_spmd(nc, in_maps, core_ids=list(range(NCORES)))
    out = np.empty((4, SEQ, D), f32)
    for c in range(NCORES):
        b, hh = c // 2, c % 2
        out[b, hh * TOK:(hh + 1) * TOK, :] = np.asarray(res.results[c]["outT"]).T
    return out
```
